# Optimizing a Trainium2 kernel written in Bass

```python
import jax, jax.numpy as jnp
from jax import lax
import numpy as np

D_MODEL = 2048
BATCH = 4
SEQ = 2048
DEPTH = 2

GRID_W = 64
CTX_LEN = 256

D_MLSTM = D_MODEL // 2
N_MLSTM_HEADS = 4
MLSTM_HEAD_DIM = D_MLSTM // N_MLSTM_HEADS
MLSTM_CHUNK = 64
QK_CONV = 3
N_DIR = 2
D_POOL = D_MODEL - D_MLSTM
POOL_WINDOWS = (2, 4, 8, 16)
N_POOL_GROUPS = len(POOL_WINDOWS)
POOL_GROUP_DIM = D_POOL // N_POOL_GROUPS
N_GATE = 2 * N_DIR * N_MLSTM_HEADS
N_IN_EVEN = 5 * D_MLSTM + N_GATE + 2 * D_POOL
EVEN_SPLITS = (D_MLSTM, 2 * D_MLSTM, 3 * D_MLSTM, 4 * D_MLSTM, 5 * D_MLSTM,
               5 * D_MLSTM + N_GATE, 5 * D_MLSTM + N_GATE + D_POOL)
D_SGU = D_MODEL
N_SGU_HEADS = 8
SGU_HEAD_DIM = D_SGU // N_SGU_HEADS
SGU_CHUNK = 128
N_IN_ODD = 3 * D_SGU

N_EVEN = (DEPTH + 1) // 2
N_ODD = DEPTH // 2
DEEPNORM_ALPHA = (2.0 * DEPTH) ** 0.25
DEEPNORM_BETA = (8.0 * DEPTH) ** -0.25
LN_EPS = 1e-5

kernel_name = "hybrid_mlstm_pool_sgu_deepnorm_prefix"


def _normalize(x):
    xf = x.astype(jnp.float32)
    mu = xf.mean(-1, keepdims=True)
    var = jnp.square(xf - mu).mean(-1, keepdims=True)
    return (xf - mu) * lax.rsqrt(var + LN_EPS)


def layer_norm(x, g, b):
    return (_normalize(x) * g + b).astype(x.dtype)


def ada_params(cond, w, b):
    mod = jax.nn.silu(cond) @ w + b
    return jnp.split(mod, 3, axis=-1)


def dwconv_centered(x, w):
    k, ch = w.shape
    return lax.conv_general_dilated(x, w[:, None, :].astype(x.dtype), window_strides=(1,),
                                    padding=[(k // 2, k // 2)],
                                    dimension_numbers=('NWC', 'WIO', 'NWC'),
                                    feature_group_count=ch)


def mlstm_scan(q, k, v, log_i, log_f, state):
    bsz, nh, seqlen, dh = q.shape
    nc = seqlen // MLSTM_CHUNK

    def to_chunks(a):
        a = a.astype(jnp.float32).reshape(bsz, nh, nc, MLSTM_CHUNK, *a.shape[3:])
        return jnp.moveaxis(a, 2, 0)

    lower = jnp.tril(jnp.ones((MLSTM_CHUNK, MLSTM_CHUNK), bool))

    def step(carry, inp):
        cmat, nvec, m = carry
        qc, kc, vc, ic, fc = inp
        b = jnp.cumsum(fc, axis=-1)
        d = b[..., :, None] - b[..., None, :] + ic[..., None, :]
        d = jnp.where(lower, d, -jnp.inf)
        inter = b + m[..., None]
        m_t = jnp.maximum(d.max(-1), inter)
        w_intra = jnp.exp(d - m_t[..., None])
        w_inter = jnp.exp(inter - m_t)
        s = jnp.einsum('bhtd,bhsd->bhts', qc, kc) * w_intra
        num = (jnp.einsum('bhts,bhsd->bhtd', s, vc)
               + w_inter[..., None] * jnp.einsum('bhvk,bhtk->bhtv', cmat, qc))
        den = s.sum(-1) + w_inter * jnp.einsum('bhk,bhtk->bht', nvec, qc)
        h = num / jnp.maximum(jnp.abs(den), jnp.exp(-m_t))[..., None]
        b_last = b[..., -1]
        g = b_last[..., None] - b + ic
        m_new = jnp.maximum(b_last + m, g.max(-1))
        w_s = jnp.exp(g - m_new[..., None])
        decay = jnp.exp(b_last + m - m_new)
        c_new = decay[..., None, None] * cmat + jnp.einsum('bhs,bhsv,bhsk->bhvk', w_s, vc, kc)
        n_new = decay[..., None] * nvec + jnp.einsum('bhs,bhsk->bhk', w_s, kc)
        return (c_new, n_new, m_new), h

    xs = tuple(to_chunks(a) for a in (q, k, v, log_i, log_f))
    state, h = lax.scan(step, state, xs)
    h = jnp.moveaxis(h, 0, 2).reshape(bsz, nh, seqlen, dh)
    return h, state


def zero_mlstm_state(bsz):
    return (jnp.zeros((bsz, N_MLSTM_HEADS, MLSTM_HEAD_DIM, MLSTM_HEAD_DIM), jnp.float32),
            jnp.zeros((bsz, N_MLSTM_HEADS, MLSTM_HEAD_DIM), jnp.float32),
            jnp.zeros((bsz, N_MLSTM_HEADS), jnp.float32))


def even_proj(xm, w_in, conv_qk, b_igate, b_fgate):
    bsz, seqlen, _ = xm.shape
    proj = xm @ w_in
    q, k, v, o, z_m, gates, u_p, z_p = jnp.split(proj, EVEN_SPLITS, axis=-1)
    qk = jax.nn.silu(dwconv_centered(jnp.concatenate([q, k], -1), conv_qk))
    q, k = jnp.split(qk, 2, axis=-1)

    def heads(a):
        return a.reshape(bsz, seqlen, N_MLSTM_HEADS, MLSTM_HEAD_DIM).transpose(0, 2, 1, 3)

    q, k, v = heads(q), heads(k) * (MLSTM_HEAD_DIM ** -0.5), heads(v)
    gates = gates.astype(jnp.float32).reshape(bsz, seqlen, 2, N_DIR, N_MLSTM_HEADS)
    log_i = (gates[:, :, 0] + b_igate).transpose(2, 0, 3, 1)
    log_f = jax.nn.log_sigmoid(gates[:, :, 1] + b_fgate).transpose(2, 0, 3, 1)
    return q, k, v, o, z_m, log_i, log_f, u_p, z_p


def multiscale_pool(u, row_len):
    bsz, seqlen, _ = u.shape
    rows = seqlen // row_len
    ug = u.astype(jnp.float32).reshape(bsz, rows, row_len, N_POOL_GROUPS, POOL_GROUP_DIM)
    cs = jnp.pad(jnp.cumsum(ug, axis=2), ((0, 0), (0, 0), (1, 0), (0, 0), (0, 0)))
    pos = jnp.arange(row_len)
    outs = []
    for g, w in enumerate(POOL_WINDOWS):
        lo = jnp.maximum(pos - w // 2, 0)
        hi = jnp.minimum(pos + (w - 1 - w // 2), row_len - 1)
        csg = cs[:, :, :, g, :]
        s = jnp.take(csg, hi + 1, axis=2) - jnp.take(csg, lo, axis=2)
        outs.append(s / (hi - lo + 1).astype(jnp.float32)[:, None])
    pooled = jnp.stack(outs, axis=3)
    return (pooled - ug).reshape(bsz, seqlen, N_POOL_GROUPS, POOL_GROUP_DIM)


def even_out(h, o, z_m, u_p, z_p, row_len, mh_norm_g, pool_w, pool_scale):
    bsz, nh, seqlen, dh = h.shape
    h = jnp.moveaxis(h, 1, 2) * jax.nn.sigmoid(o.astype(jnp.float32)).reshape(bsz, seqlen, nh, dh)
    h = _normalize(h).reshape(bsz, seqlen, D_MLSTM) * mh_norm_g
    y_m = h * jax.nn.silu(z_m.astype(jnp.float32))
    r = multiscale_pool(u_p, row_len)
    y_p = jnp.einsum('blgc,gcd->blgd', r, pool_w.astype(jnp.float32)).reshape(bsz, seqlen, D_POOL)
    y_p = y_p * pool_scale * jax.nn.silu(z_p.astype(jnp.float32))
    return jnp.concatenate([y_m, y_p], axis=-1).astype(u_p.dtype)


def sgu_mixer(xm, w_in, ln_g, ln_b, w_sp, b_sp):
    bsz, seqlen, _ = xm.shape
    u, v, z = jnp.split(xm @ w_in, 3, axis=-1)
    v = layer_norm(v, ln_g, ln_b)
    nc = seqlen // SGU_CHUNK
    vh = v.reshape(bsz, nc, SGU_CHUNK, N_SGU_HEADS, SGU_HEAD_DIM)
    s = jnp.einsum('hts,bnshc->bnthc', w_sp, vh) + b_sp.T[None, None, :, :, None]
    return u * s.reshape(bsz, seqlen, D_SGU) * jax.nn.silu(z)


def setup_inputs(seed: int = 0) -> dict:
    key = jax.random.key(seed)
    ks = jax.random.split(key, 24)

    def nrm(k, shape, s):
        return jax.random.normal(k, shape, jnp.float32) * s

    d = D_MODEL
    fgate_base = jnp.linspace(3.0, 6.0, N_MLSTM_HEADS, dtype=jnp.float32)
    return {
        "x": nrm(ks[0], (BATCH, SEQ, d), 1.0),
        "c": nrm(ks[1], (BATCH, d), 1.0),
        "ctx": nrm(ks[2], (BATCH, CTX_LEN, d), 1.0),
        "c_ctx": nrm(ks[3], (d,), 1.0),
        "ada_w": nrm(ks[4], (DEPTH, d, 3 * d), 0.5 * d ** -0.5),
        "ada_b": nrm(ks[5], (DEPTH, 3 * d), 0.02),
        "post_ln_g": 1.0 + nrm(ks[6], (DEPTH, d), 0.05),
        "post_ln_b": nrm(ks[7], (DEPTH, d), 0.02),
        "w_in_even": nrm(ks[8], (N_EVEN, d, N_IN_EVEN), d ** -0.5),
        "conv_qk": nrm(ks[9], (N_EVEN, QK_CONV, 2 * D_MLSTM), QK_CONV ** -0.5),
        "b_igate": nrm(ks[10], (N_EVEN, N_DIR, N_MLSTM_HEADS), 0.1),
        "b_fgate": fgate_base + nrm(ks[11], (N_EVEN, N_DIR, N_MLSTM_HEADS), 0.1),
        "mh_norm_g": 1.0 + nrm(ks[12], (N_EVEN, D_MLSTM), 0.05),
        "pool_w": nrm(ks[13], (N_EVEN, N_POOL_GROUPS, POOL_GROUP_DIM, POOL_GROUP_DIM), POOL_GROUP_DIM ** -0.5),
        "pool_scale": 1.0 + nrm(ks[14], (N_EVEN, D_POOL), 0.1),
        "w_out_even": nrm(ks[15], (N_EVEN, D_MLSTM + D_POOL, d), DEEPNORM_BETA * (D_MLSTM + D_POOL) ** -0.5),
        "w_in_odd": nrm(ks[16], (N_ODD, d, N_IN_ODD), d ** -0.5),
        "sgu_ln_g": 1.0 + nrm(ks[17], (N_ODD, D_SGU), 0.05),
        "sgu_ln_b": nrm(ks[18], (N_ODD, D_SGU), 0.02),
        "w_sp": nrm(ks[19], (N_ODD, N_SGU_HEADS, SGU_CHUNK, SGU_CHUNK), SGU_CHUNK ** -0.5),
        "b_sp": 1.0 + nrm(ks[20], (N_ODD, N_SGU_HEADS, SGU_CHUNK), 0.1),
        "w_out_odd": nrm(ks[21], (N_ODD, D_SGU, d), DEEPNORM_BETA * D_SGU ** -0.5),
    }


def reference(x, c, ctx, c_ctx, ada_w, ada_b, post_ln_g, post_ln_b, w_in_even, conv_qk, b_igate,
              b_fgate, mh_norm_g, pool_w, pool_scale, w_out_even, w_in_odd, sgu_ln_g, sgu_ln_b,
              w_sp, b_sp, w_out_odd):
    x_lat, x_ctx = x, ctx
    bsz = x.shape[0]
    for layer in range(DEPTH):
        last = layer == DEPTH - 1
        j = layer // 2
        sh_l, sc_l, g_l = ada_params(c, ada_w[layer], ada_b[layer])
        sh_l, sc_l, g_l = sh_l[:, None], sc_l[:, None], g_l[:, None]
        sh_c, sc_c, g_c = ada_params(c_ctx, ada_w[layer], ada_b[layer])
        xm_l = x_lat * (1.0 + sc_l) + sh_l
        xm_c = x_ctx * (1.0 + sc_c) + sh_c
        if layer % 2 == 0:
            ql, kl, vl, ol, zml, lil, lfl, upl, zpl = even_proj(xm_l, w_in_even[j], conv_qk[j], b_igate[j], b_fgate[j])
            qc, kc, vc, oc, zmc, lic, lfc, upc, zpc = even_proj(xm_c, w_in_even[j], conv_qk[j], b_igate[j], b_fgate[j])
            flip = lambda a: jnp.flip(a, axis=2)
            zero = zero_mlstm_state(bsz)
            hc_f, st_f = mlstm_scan(qc, kc, vc, lic[0], lfc[0], zero)
            hl_f, _ = mlstm_scan(ql, kl, vl, lil[0], lfl[0], st_f)
            hc_b, st_b = mlstm_scan(flip(qc), flip(kc), flip(vc), flip(lic[1]), flip(lfc[1]), zero)
            hl_b, _ = mlstm_scan(flip(ql), flip(kl), flip(vl), flip(lil[1]), flip(lfl[1]), st_b)
            y_l = even_out(hl_f + flip(hl_b), ol, zml, upl, zpl, GRID_W, mh_norm_g[j], pool_w[j], pool_scale[j])
            x_lat_new = layer_norm(DEEPNORM_ALPHA * x_lat + g_l * (y_l @ w_out_even[j]), post_ln_g[layer], post_ln_b[layer])
            if not last:
                y_c = even_out(hc_f + flip(hc_b), oc, zmc, upc, zpc, x_ctx.shape[1], mh_norm_g[j], pool_w[j], pool_scale[j])
                x_ctx = layer_norm(DEEPNORM_ALPHA * x_ctx + g_c * (y_c @ w_out_even[j]), post_ln_g[layer], post_ln_b[layer])
            x_lat = x_lat_new
        else:
            y_l = sgu_mixer(xm_l, w_in_odd[j], sgu_ln_g[j], sgu_ln_b[j], w_sp[j], b_sp[j])
            x_lat = layer_norm(DEEPNORM_ALPHA * x_lat + g_l * (y_l @ w_out_odd[j]), post_ln_g[layer], post_ln_b[layer])
            if not last:
                y_c = sgu_mixer(xm_c, w_in_odd[j], sgu_ln_g[j], sgu_ln_b[j], w_sp[j], b_sp[j])
                x_ctx = layer_norm(DEEPNORM_ALPHA * x_ctx + g_c * (y_c @ w_out_odd[j]), post_ln_g[layer], post_ln_b[layer])
    return x_lat
```

```python
import numpy as np
import concourse.bass as bass
import concourse.mybir as mybir
from concourse.bass_utils import run_bass_kernel_spmd

F32 = mybir.dt.float32
BF16 = mybir.dt.bfloat16
U8 = mybir.dt.uint8
ALU = mybir.AluOpType
AF = mybir.ActivationFunctionType

COMPUTE = ("pe", "act", "dve", "pool")
ALPHA = 4.0 ** 0.25
EPS = 1e-5
NBLK0 = 18
NBLK1 = 16


class Prog:
    def __init__(self, nc):
        self.nc = nc
        self.streams = {e: [] for e in COMPUTE + ("sp",)}
        self.count = {e: 0 for e in COMPUTE}
        self.last_w = {}
        self.readers = {}
        self.dma_count = {}
        self.final_tokens = []
        self.barrier_deps = {}
        self.groups = {}
        self.ps_rr = 0

    def next_ps(self):
        i = self.ps_rr
        self.ps_rr = (i + 1) % 8
        return i

    def _deps(self, reads, writes, stream, gwrites=()):
        deps = set()
        for k in reads:
            if k in self.last_w:
                deps.add(self.last_w[k])
            for g in self.groups.get(k, ()):
                deps.add(g)
        for k in gwrites:
            if k in self.last_w:
                deps.add(self.last_w[k])
            for r in self.readers.get(k, ()):
                deps.add(r)
        for k in writes:
            if k in self.last_w:
                deps.add(self.last_w[k])
            for r in self.readers.get(k, ()):
                deps.add(r)
            for g in self.groups.get(k, ()):
                deps.add(g)
        if stream in self.barrier_deps:
            deps |= self.barrier_deps.pop(stream)
        return deps

    def _commit(self, token, reads, writes, gwrites=()):
        for k in reads:
            self.readers.setdefault(k, []).append(token)
        for k in writes:
            self.last_w[k] = token
            self.readers[k] = []
            self.groups.pop(k, None)
        for k in gwrites:
            self.groups.setdefault(k, []).append(token)

    @staticmethod
    def _split(reads, writes):
        r2 = [k for k in reads if not k.startswith("ps")]
        w2 = list(writes) + [k for k in reads if k.startswith("ps")]
        return r2, w2

    def op(self, eng, fn, reads=(), writes=(), gwrites=()):
        reads, writes = self._split(reads, writes)
        deps = self._deps(reads, writes, eng, gwrites)
        self.count[eng] += 1
        token = ("e:" + eng, self.count[eng])
        self.streams[eng].append((deps, fn, token))
        self._commit(token, reads, writes, gwrites)
        return token

    def dma(self, queue, slot, fn, reads=(), writes=(), final=False):
        deps = self._deps(reads, writes, queue)
        self.dma_count[slot] = self.dma_count.get(slot, 0) + 16
        token = ("d:" + slot, self.dma_count[slot])
        self.streams[queue].append((deps, fn, token))
        self._commit(token, reads, writes)
        if final:
            self.final_tokens.append(token)
        return token

    def barrier(self):
        self.marks = getattr(self, "marks", []) + [dict(self.count)]
        toks = set()
        for e in COMPUTE:
            if self.count[e]:
                toks.add(("e:" + e, self.count[e]))
        for s, v in self.dma_count.items():
            toks.add(("d:" + s, v))
        for s in self.streams:
            self.barrier_deps[s] = set(toks) | self.barrier_deps.get(s, set())

    def finalize(self):
        from contextlib import ExitStack
        nc = self.nc
        with ExitStack() as es:
            sems = {}
            for e in COMPUTE:
                sems["e:" + e] = es.enter_context(nc.semaphore("sem_" + e))
            for s in self.dma_count:
                sems["d:" + s] = es.enter_context(nc.semaphore("dsem_" + s))
            block = es.enter_context(nc.Block())
            final_tokens = list(self.final_tokens)
            streams = self.streams

            def emit(stream_name, eng):
                waited = {}
                for deps, fn, token in streams[stream_name]:
                    for (sname, val) in sorted(deps):
                        if sname == "e:pe" and stream_name == "pe":
                            continue
                        if waited.get(sname, 0) >= val:
                            continue
                        eng.wait_ge(sems[sname], val)
                        waited[sname] = val
                    inst = fn(eng)
                    inst.then_inc(sems[token[0]], 1 if token[0].startswith("e:") else 16)
                if stream_name == "sp":
                    fin = {}
                    for (sname, val) in final_tokens:
                        fin[sname] = max(fin.get(sname, 0), val)
                    for (sname, val) in sorted(fin.items()):
                        if waited.get(sname, 0) < val:
                            eng.wait_ge(sems[sname], val)
                            waited[sname] = val

            @block.sync
            def _(e):
                emit("sp", e)

            @block.tensor
            def _(e):
                emit("pe", e)

            @block.scalar
            def _(e):
                emit("act", e)

            @block.vector
            def _(e):
                emit("dve", e)

            @block.gpsimd
            def _(e):
                emit("pool", e)


class Arena:
    def __init__(self, nc, nbytes):
        self.nc = nc
        self.cm = nc.sbuf_tensor("arena", [128, nbytes], U8)
        self.cm.__enter__()
        self.base = list(nc.allocations)[-1].memorylocations[0].addr
        self.size = nbytes
        self.top = 0
        self.n = 0

    def alloc(self, shape, dtype, at=None):
        nb = int(np.prod(shape[1:])) * (4 if dtype == F32 else 2)
        nb = (nb + 63) // 64 * 64
        if at is None:
            off = self.top
            self.top += nb
        else:
            off = at
        assert off + nb <= self.size, ("SBUF arena overflow", off, nb, self.size)
        self.n += 1
        return self.nc.alloc_sbuf_tensor_at("t%d" % self.n, list(shape), dtype, offset=self.base + off)


DEBUG = []


def build(layers):
    nc = bass.Bass("TRN2", target_bir_lowering=False)
    P = Prog(nc)
    dbg_n = [0]

    def dump(name, ap, shape, keys):
        if name not in DEBUG:
            return
        d = nc.dram_tensor("dbg_" + name, list(shape), F32, kind="ExternalOutput").ap()
        dbg_n[0] += 1
        q = "pool" if ap.dtype != F32 else "sp"
        P.dma(q, "dbg%d" % dbg_n[0], lambda e: e.dma_start(out=d, in_=ap), reads=keys, final=True)
    L0 = 0 in layers
    L1 = 1 in layers

    def din(name, shape):
        return nc.dram_tensor(name, list(shape), F32, kind="ExternalInput").ap()

    xo = din("xo", [1024, 2048])
    cc_d = din("cc", [128, 32])
    adaw_d = din("adaw", [12 * len(layers), 128, 8192])
    adabf_d = din("adabf", [128, len(layers) * 96])
    ident_d = din("ident", [128, 128])
    rows_d = din("rowsb", [10, 128, 2048])
    WB0 = 0
    WB1 = NBLK0 if L0 else 0
    AD1 = 24 if L0 else 0
    wst_d = din("wst", [(NBLK0 if L0 else 0) + (NBLK1 if L1 else 0), 128, 8192])
    if L0:
        xoT_d = din("xoT", [8, 128, 2048])
        xothT_d = din("xothT", [8, 128, 2048])
        xcT_d = din("xcT", [2, 128, 2048])
        wg_d = din("wg", [128, 256])
        gb_d = din("gbias", [128, 16])
        convw_d = din("convw", [128, 48])
        tri_d = din("tri", [4, 128, 128])
        pm_d = din("pmT", [4, 128, 128])
        poolw_d = din("poolw", [128, 2048])
    if L1:
        wsp_d = din("wspT", [128, 1024])
        bsp_d = din("bsp", [128, 8])
    out_d = nc.dram_tensor("out", [1024, 2048], F32, kind="ExternalOutput").ap()

    A = Arena(nc, 212800)
    ps = []
    for i in range(8):
        cm = nc.psum_tensor("psb%d" % i, [128, 512], F32)
        ps.append(cm.__enter__())

    ident_f = A.alloc([128, 128], F32)
    ident_b = A.alloc([128, 128], BF16)
    ones_f = A.alloc([128, 128], F32)
    cc = A.alloc([128, 16, 2], F32)
    csil = A.alloc([128, 16, 2], F32)
    adabf = A.alloc([128, len(layers), 96], F32)
    mods = {l: A.alloc([128, 48, 2], F32) for l in layers}
    sc1ps = {l: A.alloc([128, 16, 2], F32) for l in layers}
    mod = mods[layers[0]]
    small = A.alloc([128, 64], F32)
    stats = A.alloc([128, 16, 6], F32)
    XR_OFF = A.size - 65536
    XR = A.alloc([128, 8, 2048], F32, at=XR_OFF)
    P.dma("sp", "c0", lambda e: e.dma_start(out=ident_f[:], in_=ident_d[:, :]), writes=["ident_f"])
    P.dma("sp", "c1", lambda e: e.dma_start(out=cc[:].rearrange("p a b -> p (a b)"), in_=cc_d[:, :]), writes=["cc"])
    P.dma("sp", "c2", lambda e: e.dma_start(out=adabf[:].rearrange("p a b -> p (a b)"), in_=adabf_d[:, :]), writes=["adabf"])
    P.op("dve", lambda e: e.tensor_copy(out=ident_b[:], in_=ident_f[:]), reads=["ident_f"], writes=["ident_b"])
    P.op("dve", lambda e: e.memset(ones_f[:], 1.0), writes=["ones_f"])
    P.op("act", lambda e: e.activation(out=csil[:], in_=cc[:], func=AF.Silu), reads=["cc"], writes=["csil"])
    base_top = A.top

    def wload(wb, key, blk, c0=0, ncols=512, d0=None):
        d0 = c0 if d0 is None else d0
        src = wst_d[blk].rearrange("p (k n) -> p k n", k=16)[:, :, c0:c0 + ncols]
        P.dma("pool", "d" + key, lambda e: e.dma_start(out=wb[:, :, d0:d0 + ncols], in_=src), writes=[key])

    def run_il(gens, width=2, bg=None):
        gens = list(gens)
        bg = list(bg) if bg else []
        active = []
        cur_bg = None
        while gens or active or bg or cur_bg is not None:
            while gens and len(active) < width:
                active.append(gens.pop(0))
            for g_ in list(active):
                try:
                    next(g_)
                except StopIteration:
                    active.remove(g_)
            if cur_bg is None and bg:
                cur_bg = bg.pop(0)
            if cur_bg is not None:
                try:
                    next(cur_bg)
                except StopIteration:
                    cur_bg = None

    class WStream:
        def __init__(self, bufs, seq):
            self.bufs, self.seq = bufs, seq
            self.issued = self.consumed = self.released = 0
            self.done()

        def _issue(self):
            wb, key = self.bufs[self.issued % len(self.bufs)]
            wload(wb, key, *self.seq[self.issued])
            self.issued += 1

        def next(self):
            if self.consumed == self.issued:
                assert self.issued - self.released < len(self.bufs)
                self._issue()
            wb, key = self.bufs[self.consumed % len(self.bufs)]
            self.consumed += 1
            return wb, key

        def done(self):
            self.released = self.consumed
            while self.issued < len(self.seq) and self.issued - self.released < len(self.bufs):
                self._issue()

    cur = {}

    ada_t = {}

    def ada_setup():
        m0 = A.top
        A.top = 134 * 1024
        ada_t["abufs"] = [(A.alloc([128, 16, 512], BF16), "ab%d" % i) for i in range(3)]
        ada_t["csil_b"] = A.alloc([128, 16, 2], BF16)
        ada_t["modrow"] = A.alloc([2, 6144], F32)
        ada_t["n"] = 0
        A.top = m0
        csil_b = ada_t["csil_b"]
        P.op("dve", lambda e: e.tensor_copy(out=csil_b[:], in_=csil[:]), reads=["csil"], writes=["csil_b"])

    def ada_gen(li, layer):
        abufs, csil_b, modrow = ada_t["abufs"], ada_t["csil_b"], ada_t["modrow"]
        mk = "modrow%d" % li
        for blk in range(12):
            ab, key = abufs[ada_t["n"] % 3]
            ada_t["n"] += 1
            P.dma("pool", "d" + key, lambda e, ab=ab, blk=blk: e.dma_start(
                out=ab[:].rearrange("p k n -> p (k n)"), in_=adaw_d[li * 12 + blk]), writes=[key])
            yield
            pi = P.next_ps()
            for kc in range(16):
                P.op("pe", lambda e, ab=ab, kc=kc, pi=pi: e.matmul(
                    ps[pi][0:2, :], lhsT=csil_b[:, kc, :], rhs=ab[:, kc, :], start=(kc == 0), stop=(kc == 15)),
                    reads=[key, "csil_b"], writes=["ps%d" % pi])
            yield
            P.op("act", lambda e, blk=blk, pi=pi: e.copy(out=modrow[:, blk * 512:(blk + 1) * 512], in_=ps[pi][0:2, :]),
                 reads=["ps%d" % pi], writes=[mk])
            yield
        pi = P.next_ps()
        for c in range(48):
            P.op("pe", lambda e, c=c, pi=pi: e.transpose(out=ps[pi][:, 2 * c:2 * c + 2], in_=modrow[0:2, c * 128:(c + 1) * 128],
                                                         identity=ident_f[0:2, 0:2]),
                 reads=[mk, "ident_f"], writes=["ps%d" % pi])
        yield
        mt, st_ = mods[layer], sc1ps[layer]
        P.op("dve", lambda e, pi=pi: e.tensor_tensor(out=mt[:].rearrange("p a b -> p (a b)"), in0=ps[pi][:, 0:96],
                                                      in1=adabf[:, li, :], op=ALU.add),
             reads=["ps%d" % pi, "adabf"], writes=["mod%d" % layer])
        yield
        P.op("dve", lambda e: e.tensor_scalar(out=st_[:], in0=mt[:, 16:32, :], scalar1=1.0, scalar2=None, op0=ALU.add),
             reads=["mod%d" % layer], writes=["sc1p%d" % layer])
        yield

    def ada(layer, g_row):
        mod_t = mods[layer]
        cur["mod"], cur["sc1p"], cur["layer"] = mod_t, sc1ps[layer], layer
        m0 = A.top
        diag = A.alloc([128, 128], F32)
        for q4 in range(4):
            pi = P.next_ps()
            for j4 in range(4):
                j = q4 * 4 + j4
                P.op("dve", lambda e, j=j: e.tensor_scalar(out=diag[:], in0=ident_f[:], scalar1=mod_t[:, 32 + j, 0:1],
                                                            scalar2=None, op0=ALU.mult),
                     reads=["ident_f", "mod%d" % layer], writes=["diag"])
                P.op("pe", lambda e, j4=j4, pi=pi: e.matmul(ps[pi][:, j4 * 128:(j4 + 1) * 128], lhsT=ones_f[:], rhs=diag[:],
                                                            start=True, stop=True),
                     reads=["ones_f", "diag"], writes=["ps%d" % pi])
            P.op("act", lambda e, q4=q4, pi=pi: e.copy(out=g_row[:, q4 * 512:(q4 + 1) * 512], in_=ps[pi][:]),
                 reads=["ps%d" % pi], writes=["g_row"])
        P.barrier()
        A.top = m0

    def make_xmT(src, src_key, dst, dst_key, col0, which, keep=None):
        lo, hi = keep if keep is not None else (0, 128)
        mod, sc1p = cur["mod"], cur["sc1p"]
        mkey, skey = "mod%d" % cur["layer"], "sc1p%d" % cur["layer"]
        for q4 in range(4):
            pi = P.next_ps()
            for j4 in range(4):
                kc = q4 * 4 + j4
                P.op("pe", lambda e, kc=kc, j4=j4, pi=pi: e.transpose(
                    out=ps[pi][:, j4 * 128:(j4 + 1) * 128], in_=src[:, kc * 128:(kc + 1) * 128], identity=ident_f[:]),
                    reads=[src_key, "ident_f"], writes=["ps%d" % pi])
            for j4 in range(4):
                kc = q4 * 4 + j4
                eng = "act" if j4 % 2 == 0 else "dve"
                o_ap = dst[:, kc, col0:col0 + (hi - lo)]
                i_ap = ps[pi][:, j4 * 128 + lo:j4 * 128 + hi]
                if eng == "act":
                    P.op("act", lambda e, o_ap=o_ap, i_ap=i_ap, kc=kc: e.activation(
                        out=o_ap, in_=i_ap, func=AF.Identity, scale=sc1p[:, kc, which:which + 1],
                        bias=mod[:, kc, which:which + 1]), reads=["ps%d" % pi, skey, mkey], gwrites=[dst_key])
                else:
                    P.op("dve", lambda e, o_ap=o_ap, i_ap=i_ap, kc=kc: e.tensor_scalar(
                        out=o_ap, in0=i_ap, scalar1=sc1p[:, kc, which:which + 1], scalar2=mod[:, kc, which:which + 1],
                        op0=ALU.mult, op1=ALU.add), reads=["ps%d" % pi, skey, mkey], gwrites=[dst_key])

    def make_xmT_pre(src, src_key, dst, dst_key, col0, which, keep=None, extra_key=None):
        lo, hi = keep if keep is not None else (0, 128)
        mod, sc1p = cur["mod"], cur["sc1p"]
        mkey, skey = "mod%d" % cur["layer"], "sc1p%d" % cur["layer"]
        for kc in range(16):
            o_ap = dst[:, kc, col0:col0 + (hi - lo)]
            i_ap = src[:, kc, lo:hi]
            if kc % 3 == 0:
                P.op("act", lambda e, o_ap=o_ap, i_ap=i_ap, kc=kc: e.activation(
                    out=o_ap, in_=i_ap, func=AF.Identity, scale=sc1p[:, kc, which:which + 1],
                    bias=mod[:, kc, which:which + 1]), reads=[src_key, skey, mkey], gwrites=[dst_key] + ([extra_key] if extra_key else []))
            else:
                P.op("dve", lambda e, o_ap=o_ap, i_ap=i_ap, kc=kc: e.tensor_scalar(
                    out=o_ap, in0=i_ap, scalar1=sc1p[:, kc, which:which + 1], scalar2=mod[:, kc, which:which + 1],
                    op0=ALU.mult, op1=ALU.add), reads=[src_key, skey, mkey], gwrites=[dst_key] + ([extra_key] if extra_key else []))

    def load_rows(rb, key, r):
        P.dma("sp", "d" + key, lambda e: e.dma_start(out=rb[:], in_=rows_d[r]), writes=[key])

    def layer_norm_tile(t, grow_g, grow_b, tmp, store):
        xk = "XR%d" % t
        for c in range(4):
            P.op("dve", lambda e, c=c: e.bn_stats(out=stats[:, c, :], in_=XR[:, t, c * 512:(c + 1) * 512]),
                 reads=[xk], writes=["stats"])
        P.op("dve", lambda e: e.bn_aggr(out=small[:, 0:2], in_=stats[:, 0:4, :].rearrange("p a b -> p (a b)")),
             reads=["stats"], writes=["small"])
        P.op("act", lambda e: e.activation(out=small[:, 2:3], in_=small[:, 1:2], func=AF.Sqrt, bias=EPS),
             reads=["small"], writes=["small"])
        P.op("dve", lambda e: e.reciprocal(out=small[:, 3:4], in_=small[:, 2:3]), reads=["small"], writes=["small"])
        P.op("dve", lambda e: e.scalar_tensor_tensor(out=small[:, 4:5], in0=small[:, 0:1], scalar=-1.0, in1=small[:, 3:4],
                                                     op0=ALU.mult, op1=ALU.mult), reads=["small"], writes=["small"])
        P.op("act", lambda e: e.activation(out=XR[:, t, :], in_=XR[:, t, :], func=AF.Identity, scale=small[:, 3:4],
                                           bias=small[:, 4:5]), reads=["small"], writes=[xk])
        P.op("dve", lambda e: e.tensor_tensor(out=XR[:, t, :], in0=XR[:, t, :], in1=grow_g[:], op=ALU.mult),
             reads=["rowg"], writes=[xk])
        P.op("dve", lambda e: e.tensor_tensor(out=XR[:, t, :], in0=XR[:, t, :], in1=grow_b[:], op=ALU.add),
             reads=["rowb"], writes=[xk])
        if store:
            P.dma("sp", "st%d" % (t % 2), lambda e: e.dma_start(out=out_d[t * 128:(t + 1) * 128, :], in_=XR[:, t, :]),
                  reads=[xk], final=True)

    def out_proj(WS, yT, ykey, tiles, g_row, tmp512, after_tile=None):
        def unit(nb, ti, t, wb, wkey):
            pi = P.next_ps()
            for kc in range(16):
                P.op("pe", lambda e, kc=kc: e.matmul(
                    ps[pi][:], lhsT=yT[:, kc, ti * 128:(ti + 1) * 128], rhs=wb[:, kc, :],
                    start=(kc == 0), stop=(kc == 15)), reads=[ykey % ti, wkey], writes=["ps%d" % pi])
            t5, t5k = tmp512[(nb + ti) % 2], "tmp512_%d" % ((nb + ti) % 2)
            P.op("dve", lambda e: e.tensor_tensor(out=t5[:], in0=ps[pi][:], in1=g_row[:, nb * 512:(nb + 1) * 512],
                                                  op=ALU.mult), reads=["ps%d" % pi, "g_row"], writes=[t5k])
            P.op("dve", lambda e: e.scalar_tensor_tensor(
                out=XR[:, t, nb * 512:(nb + 1) * 512], in0=XR[:, t, nb * 512:(nb + 1) * 512], scalar=ALPHA,
                in1=t5[:], op0=ALU.mult, op1=ALU.add), reads=[t5k], writes=["XR%d" % t])

        for nb in range(2):
            wb, wkey = WS.next()
            for ti, t in enumerate(tiles):
                unit(nb, ti, t, wb, wkey)
            WS.done()
        w2, k2 = WS.next()
        w3, k3 = WS.next()
        for ti, t in enumerate(tiles):
            unit(2, ti, t, w2, k2)
            unit(3, ti, t, w3, k3)
            if after_tile is not None:
                after_tile(t)
        WS.done()

    if L0:
        A.top = base_top
        g_row = A.alloc([128, 2048], F32)
        mg = A.top
        wg = A.alloc([128, 16, 16], BF16)
        gbias = A.alloc([128, 16], F32)
        convw = A.alloc([128, 16, 3], F32)
        tri = A.alloc([128, 4, 128], F32)
        pmT = A.alloc([128, 4, 128], F32)
        poolw = A.alloc([128, 4, 2, 256], BF16)
        C32 = A.alloc([128, 8, 2, 257], F32)
        C16 = A.alloc([128, 8, 2, 257], BF16)
        G = A.alloc([128, 18, 16], F32)
        EB = A.alloc([128, 18, 8], F32)
        WSg = A.alloc([128, 18, 8], F32)
        DC = A.alloc([128, 18, 8], F32)
        kw = [A.alloc([128, 256], BF16) for _ in range(4)]
        P.dma("pool", "c5", lambda e: e.dma_start(out=wg[:].rearrange("p a b -> p (a b)"), in_=wg_d[:, :]), writes=["wg"])
        P.dma("sp", "c6", lambda e: e.dma_start(out=gbias[:], in_=gb_d[:, :]), writes=["gbias"])
        P.dma("sp", "c7", lambda e: e.dma_start(out=convw[:].rearrange("p a b -> p (a b)"), in_=convw_d[:, :]), writes=["convw"])
        for i in range(4):
            P.dma("sp", "c8%d" % i, lambda e, i=i: e.dma_start(out=tri[:, i, :], in_=tri_d[i]), writes=["tri"])
            P.dma("sp", "c9%d" % i, lambda e, i=i: e.dma_start(out=pmT[:, i, :], in_=pm_d[i]), writes=["pmT"])
        P.dma("pool", "c10", lambda e: e.dma_start(out=poolw[:].rearrange("p a b c -> p (a b c)"), in_=poolw_d[:, :]), writes=["poolw"])
        P.op("dve", lambda e: e.memset(C32[:].rearrange("p a b c -> p (a b c)"), 0.0), writes=["C32"])
        m1 = A.top
    ada_setup()
    run_il([ada_gen(0, layers[0])], width=1)
    if not (L0 and L1):
        for li_ in range(1, len(layers)):
            run_il([ada_gen(li_, layers[li_])], width=1)
        P.barrier()

    if not L0:
        for t in range(8):
            P.dma("sp" if t % 2 == 0 else "act", "xr%d" % t, lambda e, t=t: e.dma_start(out=XR[:, t, :], in_=xo[t * 128:(t + 1) * 128, :]),
                  writes=["XR%d" % t])

    if L0:
        A.top = m1
        ada(0, g_row)

        def load_x(xt_i, src, xts, q="sp"):
            key = "xt%d" % xt_i
            xt_t = xts[xt_i]
            P.dma(q, "d" + key, lambda e: e.dma_start(out=xt_t[:].rearrange("p a b -> p (a b)"), in_=src), writes=[key])
            return key

        def gates_tile_g(xT, xkeys, col0, gt):
            pi = P.next_ps()
            for kc in range(16):
                P.op("pe", lambda e, kc=kc, pi=pi: e.matmul(ps[pi][:, 0:16], lhsT=xT[:, kc, col0:col0 + 128], rhs=wg[:, kc, :],
                                                            start=(kc == 0), stop=(kc == 15)),
                     reads=list(xkeys) + ["wg"], writes=["ps%d" % pi])
            yield
            gk = "G%d" % gt
            P.op("dve", lambda e, pi=pi: e.tensor_tensor(out=G[:, gt, :], in0=ps[pi][:, 0:16], in1=gbias[:], op=ALU.add),
                 reads=["ps%d" % pi, "gbias"], writes=[gk])
            yield
            P.op("act", lambda e: e.activation(out=G[:, gt, 8:16], in_=G[:, gt, 8:16], func=AF.Exp, scale=-1.0), writes=[gk])
            yield
            P.op("act", lambda e: e.activation(out=G[:, gt, 8:16], in_=G[:, gt, 8:16], func=AF.Ln, bias=1.0), writes=[gk])
            yield
            P.op("dve", lambda e: e.tensor_scalar(out=G[:, gt, 8:16], in0=G[:, gt, 8:16], scalar1=-1.0, scalar2=None, op0=ALU.mult),
                 writes=[gk])
            yield
            pj = P.next_ps()
            P.op("pe", lambda e, pj=pj: e.matmul(ps[pj][:, 0:4], lhsT=tri[:, 0, :], rhs=G[:, gt, 8:12], start=True, stop=True),
                 reads=[gk, "tri"], writes=["ps%d" % pj])
            yield
            P.op("pe", lambda e, pj=pj: e.matmul(ps[pj][:, 4:8], lhsT=tri[:, 1, :], rhs=G[:, gt, 12:16], start=True, stop=True),
                 reads=[gk, "tri"], writes=["ps%d" % pj])
            yield
            P.op("pe", lambda e, pj=pj: e.matmul(ps[pj][:, 8:16], lhsT=ones_f[:], rhs=G[:, gt, 8:16], start=True, stop=True),
                 reads=[gk, "ones_f"], writes=["ps%d" % pj])
            yield
            P.op("dve", lambda e, pj=pj: e.tensor_tensor(out=EB[:, gt, :], in0=G[:, gt, 0:8], in1=ps[pj][:, 0:8], op=ALU.subtract),
                 reads=["ps%d" % pj, gk], writes=[gk + "q"])
            yield
            P.op("dve", lambda e, pj=pj: e.tensor_tensor(out=WSg[:, gt, :], in0=EB[:, gt, :], in1=ps[pj][:, 8:16], op=ALU.add),
                 reads=["ps%d" % pj], writes=[gk + "q"])
            yield
            P.op("act", lambda e: e.activation(out=WSg[:, gt, :], in_=WSg[:, gt, :], func=AF.Exp), writes=[gk + "q"])
            yield
            P.op("act", lambda e, pj=pj: e.activation(out=DC[:, gt, :], in_=ps[pj][:, 8:16], func=AF.Exp),
                 reads=["ps%d" % pj], writes=[gk + "q"])
            yield

        def state_update_g(k_ap, kkey, v_ap, vkey, gt, d, h, kwi):
            idx = d * 4 + h
            ck = "C%d" % idx
            kwb, kwk = kw[kwi], "kw%d" % kwi
            P.op("act", lambda e: e.activation(out=kwb[:], in_=k_ap, func=AF.Identity, scale=WSg[:, gt, idx:idx + 1]),
                 reads=[kkey, "G%dq" % gt], writes=[kwk])
            yield
            for kcc in range(2):
                pa = P.next_ps()
                P.op("pe", lambda e, pa=pa, kcc=kcc: e.matmul(ps[pa][:, 0:257], lhsT=kwb[:, kcc * 128:(kcc + 1) * 128], rhs=v_ap,
                                                              start=True, stop=True),
                     reads=[kwk] + ([vkey] if isinstance(vkey, str) else list(vkey)), writes=["ps%d" % pa])
                yield
                P.op("dve", lambda e, pa=pa, kcc=kcc: e.scalar_tensor_tensor(
                    out=C32[:, idx, kcc, :], in0=C32[:, idx, kcc, :], scalar=DC[:, gt, idx:idx + 1], in1=ps[pa][:, 0:257],
                    op0=ALU.mult, op1=ALU.add), reads=["ps%d" % pa, "G%dq" % gt], writes=[ck])
                yield

        def cast_state(idx):
            P.op("act", lambda e: e.copy(out=C16[:, idx, :, :], in_=C32[:, idx, :, :]), reads=["C%d" % idx], writes=["C16_%d" % idx])

        def conv3(pre, sg, n, cwi, segs, pk="pre", sk="sg"):
            for (a, b, hl, hr) in segs:
                P.op("dve", lambda e, a=a, b=b: e.tensor_scalar(out=sg[:, a:b], in0=pre[:, a:b], scalar1=convw[:, cwi, 1:2],
                                                                  scalar2=None, op0=ALU.mult), reads=[pk, "convw"], writes=[sk])
                la = a if hl else a + 1
                P.op("dve", lambda e, la=la, b=b: e.scalar_tensor_tensor(
                    out=sg[:, la:b], in0=pre[:, la - 1:b - 1], scalar=convw[:, cwi, 0:1], in1=sg[:, la:b],
                    op0=ALU.mult, op1=ALU.add), reads=[pk, "convw"], writes=[sk])
                rb = b if hr else b - 1
                P.op("dve", lambda e, a=a, rb=rb: e.scalar_tensor_tensor(
                    out=sg[:, a:rb], in0=pre[:, a + 1:rb + 1], scalar=convw[:, cwi, 2:3], in1=sg[:, a:rb],
                    op0=ALU.mult, op1=ALU.add), reads=[pk, "convw"], writes=[sk])

        def proj_fm(wb, wkey, c0, xT, xkeys, ranges, pre, pk="pre"):
            for (a, b, o) in ranges:
                pi = P.next_ps()
                for kc in range(16):
                    P.op("pe", lambda e, kc=kc, pi=pi, a=a, b=b: e.matmul(
                        ps[pi][:, 0:b - a], lhsT=wb[:, kc, c0:c0 + 128], rhs=xT[:, kc, a:b], start=(kc == 0), stop=(kc == 15)),
                        reads=list(xkeys) + [wkey], writes=["ps%d" % pi])
                P.op("act", lambda e, pi=pi, a=a, b=b, o=o: e.copy(out=pre[:, o:o + b - a], in_=ps[pi][:, 0:b - a]),
                     reads=["ps%d" % pi], writes=[pk])

        def fm_chunk_gen(wb, wkey, c0, srcs, pre, pk, sg, sk, cwi, segs, lo, hi, is_k, out_ap, okey, tr=None):
            for (xT, xkeys, ranges) in srcs:
                for (a_, b_, o_) in ranges:
                    pi = P.next_ps()
                    for kc in range(16):
                        P.op("pe", lambda e, kc=kc, pi=pi, a_=a_, b_=b_, xT=xT: e.matmul(
                            ps[pi][:, 0:b_ - a_], lhsT=wb[:, kc, c0:c0 + 128], rhs=xT[:, kc, a_:b_], start=(kc == 0), stop=(kc == 15)),
                            reads=list(xkeys) + [wkey], writes=["ps%d" % pi])
                    P.op("act", lambda e, pi=pi, a_=a_, b_=b_, o_=o_: e.copy(out=pre[:, o_:o_ + b_ - a_], in_=ps[pi][:, 0:b_ - a_]),
                         reads=["ps%d" % pi], gwrites=[pk])
                    yield
            for (a_, b_, hl, hr) in segs:
                P.op("dve", lambda e, a_=a_, b_=b_: e.tensor_scalar(out=sg[:, a_:b_], in0=pre[:, a_:b_], scalar1=convw[:, cwi, 1:2],
                                                                      scalar2=None, op0=ALU.mult), reads=[pk, "convw"], writes=[sk])
                yield
                la = a_ if hl else a_ + 1
                P.op("dve", lambda e, la=la, b_=b_: e.scalar_tensor_tensor(
                    out=sg[:, la:b_], in0=pre[:, la - 1:b_ - 1], scalar=convw[:, cwi, 0:1], in1=sg[:, la:b_],
                    op0=ALU.mult, op1=ALU.add), reads=[pk, "convw"], writes=[sk])
                yield
                rb_ = b_ if hr else b_ - 1
                P.op("dve", lambda e, a_=a_, rb_=rb_: e.scalar_tensor_tensor(
                    out=sg[:, a_:rb_], in0=pre[:, a_ + 1:rb_ + 1], scalar=convw[:, cwi, 2:3], in1=sg[:, a_:rb_],
                    op0=ALU.mult, op1=ALU.add), reads=[pk, "convw"], writes=[sk])
                yield
            if not is_k:
                P.op("act", lambda e: e.activation(out=out_ap, in_=sg[:, lo:hi], func=AF.Silu), reads=[sk], writes=[okey])
                yield
            else:
                P.op("act", lambda e: e.activation(out=pre[:, lo:hi], in_=sg[:, lo:hi], func=AF.Sigmoid), reads=[sk], writes=[pk])
                yield
                P.op("dve", lambda e: e.scalar_tensor_tensor(out=out_ap, in0=sg[:, lo:hi], scalar=0.0625, in1=pre[:, lo:hi],
                                                             op0=ALU.mult, op1=ALU.mult), reads=[sk, pk], writes=[okey])
                yield
            if tr is not None:
                src, col0, ntile, dst_fn, dkey = tr
                for j0 in range(0, ntile, 4):
                    n = min(4, ntile - j0)
                    pi = P.next_ps()
                    psb = ps[pi][:].bitcast(BF16)
                    for j in range(n):
                        P.op("pe", lambda e, j=j, j0=j0, psb=psb: e.transpose(
                            out=psb[:, j * 128:(j + 1) * 128], in_=src[:, col0 + (j0 + j) * 128:col0 + (j0 + j + 1) * 128],
                            identity=ident_b[:]), reads=[okey, "ident_b"], writes=["ps%d" % pi])
                    P.op("act", lambda e, j0=j0, n=n, psb=psb: e.copy(
                        out=dst_fn(j0, n), in_=psb[:, 0:n * 128].rearrange("p (a b) -> p a b", a=n)),
                        reads=["ps%d" % pi], gwrites=[dkey])
                    yield

        def transposes_to(src, skey, col0, ntile, dst_fn, dkey):
            for j0 in range(0, ntile, 4):
                n = min(4, ntile - j0)
                pi = P.next_ps()
                psb = ps[pi][:].bitcast(BF16)
                for j in range(n):
                    P.op("pe", lambda e, j=j, j0=j0, psb=psb: e.transpose(
                        out=psb[:, j * 128:(j + 1) * 128], in_=src[:, col0 + (j0 + j) * 128:col0 + (j0 + j + 1) * 128],
                        identity=ident_b[:]), reads=[skey, "ident_b"], writes=["ps%d" % pi])
                P.op("act", lambda e, j0=j0, n=n, psb=psb: e.copy(
                    out=dst_fn(j0, n), in_=psb[:, 0:n * 128].rearrange("p (a b) -> p a b", a=n)),
                    reads=["ps%d" % pi], gwrites=[dkey])

        xmB = A.alloc([128, 16, 1025], BF16)
        xmC = A.alloc([128, 16, 256], BF16)
        m1x = A.top
        xts = [A.alloc([128, 16, 128], F32) for _ in range(4)]
        assert A.top <= 134 * 1024
        A.top = m1x
        ktok_oc = A.alloc([128, 10, 1024], BF16)
        vext_oc = A.alloc([128, 10, 4, 257], BF16)
        pre1b = [A.alloc([128, 1281], F32) for _ in range(2)]
        sg1b = [A.alloc([128, 1281], F32) for _ in range(2)]
        kTfb = [A.alloc([128, 1281], BF16) for _ in range(2)]
        wbufs = [(A.alloc([128, 16, 512], BF16), "w%d" % i) for i in range(2)]
        fprog = {"t": 0}

        def front_gen():
            k = load_x(0, xoT_d[7], xts)
            make_xmT_pre(xts[0], k, xmB, "xmB", 0, 0, keep=(127, 128))
            yield
            for j in range(8):
                k = load_x((j + 1) % 4, xothT_d[j], xts)
                make_xmT_pre(xts[(j + 1) % 4], k, xmB, "xmB", 1 + j * 128, 0, extra_key="xmBt%d" % j)
                fprog["t"] = j + 1
                yield
            for j in range(2):
                k = load_x((j + 1) % 4, xcT_d[j], xts)
                make_xmT_pre(xts[(j + 1) % 4], k, xmC, "xmC", j * 128, 1, extra_key="xmBt%d" % (8 + j))
                fprog["t"] = 9 + j
                yield

        def gated_gates(i):
            while fprog["t"] <= i:
                yield
            if i < 8:
                yield from gates_tile_g(xmB, ["xmBt%d" % i], 1 + i * 128, 8 + i)
            else:
                yield from gates_tile_g(xmC, ["xmBt%d" % i], (i - 8) * 128, 8 + i)

        def front_all():
            gens = [front_gen()] + [gated_gates(i) for i in range(10)]
            active = []
            while gens or active:
                while gens and len(active) < 5:
                    active.append(gens.pop(0))
                for g_ in list(active):
                    try:
                        next(g_)
                    except StopIteration:
                        active.remove(g_)
                yield

        if L1:
            run_il([ada_gen(1, 1), front_all()], width=2)
        else:
            run_il([front_all()], width=1)
        P.barrier()
        P.op("dve", lambda e: e.memset(vext_oc[:, :, :, 256:257], 1.0), writes=["vext_oc"])
        WS = WStream(wbufs, [(2, 0, 512), (3, 0, 512), (4, 0, 512), (5, 0, 512)])
        wk0 = WS.next()
        wk1 = WS.next()
        rngB = [(0, 512, 0), (512, 1024, 512), (1024, 1025, 1024)]

        def k1_chunk(f):
            wb, wkey = wk0 if f < 4 else wk1
            pre1, sg1, kTf = pre1b[f % 2], sg1b[f % 2], kTfb[f % 2]
            pk, sk, tk = "pre%d" % (f % 2), "sg%d" % (f % 2), "kTf%d" % (f % 2)
            return fm_chunk_gen(wb, wkey, (f % 4) * 128, [(xmB, ["xmB"], rngB), (xmC, ["xmC"], [(0, 256, 1025)])],
                                pre1, pk, sg1, sk, 8 + f, [(1, 1025, True, False), (1025, 1281, False, False)], 1, 1281, True,
                                kTf[:, 1:1281], tk,
                                tr=(kTf, 1, 10, lambda j0, n, f=f: ktok_oc[:, j0:j0 + n, f * 128:(f + 1) * 128], "ktok_oc"))

        run_il([k1_chunk(f) for f in range(8)], width=2)
        WS.done()
        wv0, kv0 = WS.next()
        wv1, kv1 = WS.next()

        def v_tile(tt):
            xT, xk, col = (xmB, "xmB", 1 + tt * 128) if tt < 8 else (xmC, "xmC", (tt - 8) * 128)
            for vb, (wb, wkey) in enumerate(((wv0, kv0), (wv1, kv1))):
                pi = P.next_ps()
                for kc in range(16):
                    P.op("pe", lambda e, kc=kc, pi=pi, wb=wb: e.matmul(
                        ps[pi][:], lhsT=xT[:, kc, col:col + 128], rhs=wb[:, kc, :], start=(kc == 0), stop=(kc == 15)),
                        reads=[xk, wkey], writes=["ps%d" % pi])
                P.op("act", lambda e, pi=pi, vb=vb: e.copy(out=vext_oc[:, tt, 2 * vb:2 * vb + 2, 0:256],
                                                           in_=ps[pi][:].rearrange("p (a b) -> p a b", a=2)),
                     reads=["ps%d" % pi], writes=["vext_oc%d" % tt])

        def su(d, tt, gt):
            run_il([state_update_g(ktok_oc[:, tt, h * 256:(h + 1) * 256], "ktok_oc", vext_oc[:, tt, h, :],
                                   ["vext_oc", "vext_oc%d" % tt], gt, d, h, h) for h in range(4)], width=4)

        tile_order = [8, 9, 7, 6, 5, 4, 3, 2, 1, 0]
        due = {8: [(0, 8, 16)], 9: [(0, 9, 17), (1, 9, 17), (1, 8, 16)]}
        for j in range(8):
            due[7 - j] = [(1, 7 - j, 15 - j)]
        prev = None
        for tt in tile_order:
            v_tile(tt)
            if prev is not None:
                for args in due[prev]:
                    su(*args)
            prev = tt
        for args in due[prev]:
            su(*args)
        WS.done()
        for idx in range(8):
            cast_state(idx)
        dump("C32", C32[:].rearrange("p a b c -> p (a b c)"), [128, 8 * 2 * 257], ["C%d" % i for i in range(8)])
        dump("G", G[:].rearrange("p a b -> p (a b)"), [128, 18 * 16], ["G%d" % i for i in range(8, 18)])
        P.barrier()

        A.top = m1
        yT = A.alloc([128, 16, 1024], BF16)
        m2y = A.top
        xmA = A.alloc([128, 16, 1025], BF16)
        m2 = A.top
        xts = [A.alloc([128, 16, 128], F32) for _ in range(4)]
        f2prog = {"t": 0}

        def front2_gen():
            for j in range(8):
                k = load_x(j % 4, xoT_d[j], xts, q=("sp" if j % 2 == 0 else "act"))
                make_xmT_pre(xts[j % 4], k, xmA, "xmA", j * 128, 0, extra_key="xmAt%d" % j)
                f2prog["t"] = j + 1
                yield
            k = load_x(0, xothT_d[0], xts)
            make_xmT_pre(xts[0], k, xmA, "xmA", 1024, 0, keep=(0, 1))
            yield

        def gated_gates2(i):
            while f2prog["t"] <= i:
                yield
            yield from gates_tile_g(xmA, ["xmAt%d" % i], i * 128, i)

        run_il([front2_gen()] + [gated_gates2(i) for i in range(8)], width=5)
        P.barrier()
        A.top = m2
        rowM = A.alloc([128, 2048], F32)
        load_rows(rowM, "rowM", 6)
        pre2 = [A.alloc([128, 1025], F32) for _ in range(2)]
        sg2 = [A.alloc([128, 1025], F32) for _ in range(2)]
        qT = A.alloc([128, 2, 1024], BF16)
        kT = A.alloc([128, 2, 1024], BF16)
        ktok = A.alloc([128, 8, 256], BF16)
        vext = A.alloc([128, 8, 257], BF16)
        hacc = A.alloc([128, 8, 256], F32)
        Dm2 = [A.alloc([128, 128], F32) for _ in range(2)]
        Arow2 = [A.alloc([128, 128], F32) for _ in range(2)]
        LFb2 = [A.alloc([128, 128], F32) for _ in range(2)]
        ST4 = [[A.alloc([128, 128], BF16) for _ in range(2)] for _ in range(2)]
        qA4 = [[A.alloc([128, 2, 128], BF16) for _ in range(2)] for _ in range(2)]
        hbuf2 = [A.alloc([128, 256], F32) for _ in range(2)]
        t256_2 = [A.alloc([128, 256], F32) for _ in range(2)]
        ym2 = [A.alloc([128, 256], BF16) for _ in range(2)]
        uf1 = A.alloc([128, 256], F32)
        rT1 = A.alloc([128, 2, 128], BF16)
        zs1 = A.alloc([128, 256], F32)
        wbufs = [(A.alloc([128, 16, 256], BF16), "w%d" % i) for i in range(4)]
        P.op("dve", lambda e: e.memset(vext[:, :, 256:257], 1.0), writes=["vext"])
        seq = []
        for h in range(4):
            c0 = (h % 2) * 256
            seq += [(0 + h // 2, c0, 256, 0), (2 + h // 2, c0, 256, 0), (4 + h // 2, c0, 256, 0),
                    (10 + h // 2, c0, 256, 0), (12 + h // 2, c0, 256, 0), (6 + h // 2, c0, 256, 0), (8 + h // 2, c0, 256, 0)]
        WS = WStream(wbufs, seq)

        def tail_chain(p, src_ap, src_keys, row_c0, fo, j, zb, kz):
            hb, yb = hbuf2[p], ym2[p]
            kh, ky = "hh%d" % p, "ym%d" % p
            P.op("dve", lambda e: e.tensor_tensor(out=hb[:], in0=src_ap, in1=rowM[:, row_c0:row_c0 + 256], op=ALU.mult),
                 reads=["rowM"] + list(src_keys), writes=[kh])
            yield
            P.op("dve", lambda e: e.tensor_tensor(out=yb[:], in0=hb[:], in1=zb[:], op=ALU.mult), reads=[kh, kz], writes=[ky])
            yield
            transposes_to(yb, ky, 0, 2, lambda j0, n: yT[:, fo:fo + 2, j * 128:(j + 1) * 128], "yT%d" % j)
            yield

        def proj2(po, j, wa, ka, wb2, kb2):
            for (wb_, wk2, oc) in ((wa, ka, 0), (wb2, kb2, 256)):
                for kc in range(16):
                    P.op("pe", lambda e, kc=kc, wb_=wb_, oc=oc: e.matmul(
                        ps[po][:, oc:oc + 256], lhsT=xmA[:, kc, j * 128:(j + 1) * 128], rhs=wb_[:, kc, 0:256],
                        start=(kc == 0), stop=(kc == 15)), reads=["xmA", wk2], writes=["ps%d" % po])

        def head_chain(h, j, wo_, okey, wz_, zkey):
            p = j % 2
            hb, tb = hbuf2[p], t256_2[p]
            kh, kt, ksm, kst = "hh%d" % p, "t256_%d" % p, "smallE%d" % p, "statsE%d" % p
            c = 16 + 8 * p
            po = P.next_ps()
            proj2(po, j, wo_, okey, wz_, zkey)
            yield
            P.op("act", lambda e: e.activation(out=tb[:], in_=ps[po][:, 0:256], func=AF.Sigmoid), reads=["ps%d" % po], writes=[kt])
            yield
            if h == 0:
                dump("hsum%d" % j, hacc[:, j, :], [128, 256], ["hacc%d" % j])
            P.op("dve", lambda e: e.tensor_tensor(out=hb[:], in0=hacc[:, j, :], in1=tb[:], op=ALU.mult),
                 reads=[kt, "hacc%d" % j], writes=[kh])
            yield
            P.op("act", lambda e: e.activation(out=tb[:], in_=ps[po][:, 256:512], func=AF.Silu), reads=["ps%d" % po], writes=[kt])
            yield
            P.op("dve", lambda e: e.bn_stats(out=stats[:, 4 + p, :], in_=hb[:]), reads=[kh], writes=[kst])
            yield
            P.op("dve", lambda e: e.bn_aggr(out=small[:, c:c + 2], in_=stats[:, 4 + p, :]), reads=[kst], writes=[ksm])
            yield
            P.op("act", lambda e: e.activation(out=small[:, c + 2:c + 3], in_=small[:, c + 1:c + 2], func=AF.Sqrt, bias=EPS), writes=[ksm])
            yield
            P.op("dve", lambda e: e.reciprocal(out=small[:, c + 3:c + 4], in_=small[:, c + 2:c + 3]), writes=[ksm])
            yield
            P.op("dve", lambda e: e.scalar_tensor_tensor(out=small[:, c + 4:c + 5], in0=small[:, c:c + 1], scalar=-1.0,
                                                         in1=small[:, c + 3:c + 4], op0=ALU.mult, op1=ALU.mult), writes=[ksm])
            yield
            P.op("act", lambda e: e.activation(out=hb[:], in_=hb[:], func=AF.Identity, scale=small[:, c + 3:c + 4],
                                               bias=small[:, c + 4:c + 5]), reads=[ksm], writes=[kh])
            yield
            yield from tail_chain(p, hb[:], [kh], h * 256, 2 * h, j, tb, kt)

        def pool_chain(g, j, wu_, ukey, wz_, zkey):
            p = j % 2
            ub, rb = uf1, rT1
            ku, kr = "uf", "rT"
            po = P.next_ps()
            for (wb_, wk2, oc) in ((wu_, ukey, 0), (wz_, zkey, 256)):
                for kc in range(16):
                    P.op("pe", lambda e, kc=kc, wb_=wb_, oc=oc: e.matmul(
                        ps[po][:, oc:oc + 256], lhsT=xmA[:, kc, j * 128:(j + 1) * 128], rhs=wb_[:, kc, 0:256],
                        start=(kc == 0), stop=(kc == 15)), reads=["xmA", wk2], writes=["ps%d" % po])
                yield
            P.op("act", lambda e: e.copy(out=ub[:], in_=ps[po][:, 0:256]), reads=["ps%d" % po], writes=[ku])
            yield
            P.op("act", lambda e: e.activation(out=zs1[:], in_=ps[po][:, 256:512], func=AF.Silu), reads=["ps%d" % po], writes=["zs"])
            yield
            pr = P.next_ps()
            for c2 in range(2):
                P.op("pe", lambda e, c2=c2: e.matmul(ps[pr][:, c2 * 128:(c2 + 1) * 128], lhsT=ub[:, c2 * 128:(c2 + 1) * 128],
                                                     rhs=pmT[:, g, :], start=True, stop=True), reads=[ku, "pmT"], writes=["ps%d" % pr])
            yield
            P.op("act", lambda e: e.copy(out=rb[:], in_=ps[pr][:, 0:256].rearrange("p (a b) -> p a b", a=2)),
                 reads=["ps%d" % pr], writes=[kr])
            yield
            py = P.next_ps()
            for c2 in range(2):
                P.op("pe", lambda e, c2=c2: e.matmul(ps[py][:, 0:256], lhsT=rb[:, c2, :], rhs=poolw[:, g, c2, :],
                                                     start=(c2 == 0), stop=(c2 == 1)), reads=[kr, "poolw"], writes=["ps%d" % py])
            yield
            yield from tail_chain(p, ps[py][:, 0:256], ["ps%d" % py], 1024 + g * 256, 8 + 2 * g, j, zs1, "zs")

        def scan_prep(h, d, prog):
            idx = d * 4 + h
            Dm, Arow, LFb = Dm2[d], Arow2[d], LFb2[d]
            kD, kA, kL = "Dm%d" % d, "Arow%d" % d, "LFb%d" % d
            for step in range(8):
                while step >= prog["rec%d" % d] + 2:
                    yield
                r = step % 2
                ST, qA = ST4[d][r], qA4[d][r]
                kS, kQ = "ST%d_%d" % (d, r), "qA%d_%d" % (d, r)
                j = step if d == 0 else 7 - step
                jc = slice(j * 128, (j + 1) * 128)
                P.op("act", lambda e, j=j: e.activation(out=LFb[:], in_=ones_f[:], func=AF.Identity, scale=G[:, j, 8 + idx:9 + idx]),
                     reads=["ones_f", "G%d" % j], writes=[kL])
                yield
                pb = P.next_ps()
                P.op("pe", lambda e, pb=pb: e.matmul(ps[pb][:, 0:128], lhsT=LFb[:], rhs=tri[:, d, :], start=True, stop=True),
                     reads=[kL, "tri"], writes=["ps%d" % pb])
                yield
                P.op("act", lambda e, pb=pb, j=j: e.activation(out=Dm[:], in_=ps[pb][:, 0:128], func=AF.Exp, bias=EB[:, j, idx:idx + 1]),
                     reads=["ps%d" % pb, "G%dq" % j], writes=[kD])
                yield
                P.op("act", lambda e, pb=pb: e.activation(out=Arow[:], in_=ps[pb][:, 0:128], func=AF.Exp),
                     reads=["ps%d" % pb], writes=[kA])
                yield
                P.op("dve", lambda e: e.tensor_tensor(out=Dm[:], in0=Dm[:], in1=tri[:, 2 + d, :], op=ALU.mult), reads=["tri"], writes=[kD])
                yield
                for kcc in range(2):
                    P.op("dve", lambda e, kcc=kcc, jc=jc, qA=qA: e.tensor_tensor(out=qA[:, kcc, :], in0=qT[:, kcc, jc], in1=Arow[:], op=ALU.mult),
                         reads=["qT", kA], writes=[kQ])
                    yield
                pq = P.next_ps()
                for kcc in range(2):
                    P.op("pe", lambda e, kcc=kcc, jc=jc, pq=pq: e.matmul(ps[pq][:, 0:128], lhsT=kT[:, kcc, jc], rhs=qT[:, kcc, jc],
                                                                         start=(kcc == 0), stop=(kcc == 1)),
                         reads=["kT", "qT"], writes=["ps%d" % pq])
                yield
                P.op("dve", lambda e, pq=pq, ST=ST: e.tensor_tensor(out=ST[:], in0=ps[pq][:, 0:128], in1=Dm[:], op=ALU.mult),
                     reads=["ps%d" % pq, kD], writes=[kS])
                prog["prep%d" % d] = step + 1
                yield

        def scan_rec(h, d, prog):
            idx = d * 4 + h
            kSm = "small%d" % d
            s0 = 8 + 2 * d
            for step in range(8):
                while prog["prep%d" % d] <= step:
                    yield
                r = step % 2
                ST, qA = ST4[d][r], qA4[d][r]
                kS, kQ = "ST%d_%d" % (d, r), "qA%d_%d" % (d, r)
                j = step if d == 0 else 7 - step
                ph = P.next_ps()
                P.op("pe", lambda e, ph=ph, j=j, ST=ST: e.matmul(ps[ph][:, 0:257], lhsT=ST[:], rhs=vext[:, j, :], start=True, stop=False),
                     reads=[kS, "vext"], writes=["ps%d" % ph])
                for kcc in range(2):
                    P.op("pe", lambda e, ph=ph, kcc=kcc, qA=qA: e.matmul(ps[ph][:, 0:257], lhsT=qA[:, kcc, :], rhs=C16[:, idx, kcc, :],
                                                                         start=False, stop=(kcc == 1)),
                         reads=[kQ, "C16_%d" % idx], writes=["ps%d" % ph])
                prog["rec%d" % d] = step + 1
                yield
                P.op("dve", lambda e, ph=ph: e.tensor_scalar(out=small[:, s0:s0 + 1], in0=ps[ph][:, 256:257], scalar1=-1.0, scalar2=None,
                                                             op0=ALU.mult), reads=["ps%d" % ph], writes=[kSm])
                yield
                P.op("dve", lambda e, ph=ph: e.scalar_tensor_tensor(out=small[:, s0:s0 + 1], in0=small[:, s0:s0 + 1], scalar=1.0,
                                                                    in1=ps[ph][:, 256:257], op0=ALU.max, op1=ALU.max),
                     reads=["ps%d" % ph], writes=[kSm])
                yield
                P.op("dve", lambda e: e.reciprocal(out=small[:, s0 + 1:s0 + 2], in_=small[:, s0:s0 + 1]), writes=[kSm])
                yield
                first = j not in prog["hw"]
                prog["hw"].add(j)
                if first:
                    P.op("act", lambda e, ph=ph, j=j: e.activation(out=hacc[:, j, :], in_=ps[ph][:, 0:256], func=AF.Identity,
                                                                    scale=small[:, s0 + 1:s0 + 2]),
                         reads=["ps%d" % ph, kSm], writes=["hacc%d" % j])
                else:
                    P.op("dve", lambda e, ph=ph, j=j: e.scalar_tensor_tensor(
                        out=hacc[:, j, :], in0=ps[ph][:, 0:256], scalar=small[:, s0 + 1:s0 + 2], in1=hacc[:, j, :],
                        op0=ALU.mult, op1=ALU.add), reads=["ps%d" % ph, kSm], writes=["hacc%d" % j])
                yield
                if step < 7:
                    yield from state_update_g(ktok[:, j, :], "ktok", vext[:, j, :], "vext", j, d, h, d)
                    cast_state(idx)
                    yield

        rng3 = [(0, 512, 0), (512, 1024, 512), (1024, 1025, 1024)]
        for h in range(4):
            wq, qkey = WS.next()
            wk_, kkey = WS.next()

            def qk_chunk(i):
                f2 = i % 2
                pre, pk, sgx, sk = pre2[f2], "pre%d" % f2, sg2[f2], "sgh%d" % f2
                if i < 2:
                    return fm_chunk_gen(wq, qkey, f2 * 128, [(xmA, ["xmA"], rng3)], pre, pk, sgx, sk, 2 * h + f2,
                                        [(0, 1024, False, True)], 0, 1024, False, qT[:, f2, :], "qT")
                return fm_chunk_gen(wk_, kkey, f2 * 128, [(xmA, ["xmA"], rng3)], pre, pk, sgx, sk, 8 + 2 * h + f2,
                                    [(0, 1024, False, True)], 0, 1024, True, kT[:, f2, :], "kT",
                                    tr=(kT[:, f2, :], 0, 8, lambda j0, n, f2=f2: ktok[:, j0:j0 + n, f2 * 128:(f2 + 1) * 128], "ktok"))

            run_il([qk_chunk(i) for i in range(4)], width=2)
            WS.done()
            wv, vkey = WS.next()
            for j in range(8):
                pi = P.next_ps()
                for kc in range(16):
                    P.op("pe", lambda e, kc=kc, pi=pi, j=j, wv=wv: e.matmul(
                        ps[pi][:, 0:256], lhsT=xmA[:, kc, j * 128:(j + 1) * 128], rhs=wv[:, kc, 0:256],
                        start=(kc == 0), stop=(kc == 15)), reads=["xmA", vkey], writes=["ps%d" % pi])
                P.op("act", lambda e, pi=pi, j=j: e.copy(out=vext[:, j, 0:256], in_=ps[pi][:, 0:256]), reads=["ps%d" % pi], writes=["vext"])
            WS.done()
            wu_, ukey = WS.next()
            wz_, zkey = WS.next()
            prog = {"prep0": 0, "prep1": 0, "rec0": 0, "rec1": 0, "hw": set()}
            run_il([scan_prep(h, 0, prog), scan_prep(h, 1, prog), scan_rec(h, 0, prog), scan_rec(h, 1, prog)], width=4,
                   bg=[pool_chain(h, j, wu_, ukey, wz_, zkey) for j in range(8)])
            WS.done()
            wo_, okey = WS.next()
            wz_, zkey = WS.next()
            run_il([head_chain(h, j, wo_, okey, wz_, zkey) for j in range(8)])
            WS.done()
        dump("yT", yT[:].rearrange("p a b -> p (a b)"), [128, 16 * 1024], ["yT%d" % j for j in range(8)])
        P.barrier()
        A.top = mg
        rowA = A.alloc([128, 2048], F32)
        rowB = A.alloc([128, 2048], F32)
        assert A.top <= m1, A.top
        A.top = m2y
        tmp512 = [A.alloc([128, 512], F32) for _ in range(2)]
        wbufs = [(A.alloc([128, 16, 512], BF16), "w%d" % i) for i in range(3)]
        assert A.top <= XR_OFF, A.top
        for t in range(8):
            P.dma("sp" if t % 2 == 0 else "act", "xr%d" % t, lambda e, t=t: e.dma_start(out=XR[:, t, :], in_=xo[t * 128:(t + 1) * 128, :]),
                  writes=["XR%d" % t])
        load_rows(rowA, "rowg", 0)
        load_rows(rowB, "rowb", 1)
        WS = WStream(wbufs, [(14 + nb, 0, 512) for nb in range(4)])
        out_proj(WS, yT, "yT%d", list(range(8)), g_row, tmp512,
                 after_tile=lambda t: layer_norm_tile(t, rowA, rowB, None, store=False))
        P.barrier()

    if L1:
        A.top = base_top
        g_row = A.alloc([128, 2048], F32)
        ada(1, g_row)
        dump("mod1", mods[1][:].rearrange("p a b -> p (a b)"), [128, 96], ["mod"])
        dump("grow1", g_row[:], [128, 2048], ["g_row"])
        rowA = A.alloc([128, 2048], F32)
        rowB = A.alloc([128, 2048], F32)
        wspT = A.alloc([128, 8, 128], BF16)
        bsp = A.alloc([128, 8], F32)
        P.dma("pool", "c3", lambda e: e.dma_start(out=wspT[:].rearrange("p a b -> p (a b)"), in_=wsp_d[:, :]), writes=["wspT"])
        P.dma("sp", "c4", lambda e: e.dma_start(out=bsp[:], in_=bsp_d[:, :]), writes=["bsp"])
        wbufs = [(A.alloc([128, 16, 512], BF16), "w%d" % i) for i in range(3)]
        xg = A.alloc([128, 16, 512], BF16)
        VS = A.alloc([128, 4, 2048], F32)
        vln2 = [A.alloc([128, 2048], BF16) for _ in range(2)]
        tmp512 = [A.alloc([128, 512], F32) for _ in range(2)]
        assert A.top <= XR_OFF, A.top
        for gi in range(2):
            tiles = [4 * gi + i for i in range(4)]
            seq = [(WB1 + 4 + nb, 0, 512) for nb in range(4)]
            for nb in range(4):
                seq += [(WB1 + nb, 0, 512), (WB1 + 8 + nb, 0, 512)]
            seq += [(WB1 + 12 + nb, 0, 512) for nb in range(4)]
            WS = WStream(wbufs, seq)
            for ti, t in enumerate(tiles):
                make_xmT(XR[:, t, :], "XR%d" % t, xg, "xg%d" % ti, ti * 128, 0)
            if gi == 0:
                dump("xg", xg[:].rearrange("p a b -> p (a b)"), [128, 8192], ["xg0", "xg1", "xg2", "xg3"])
            load_rows(rowA, "rowg", 4)
            load_rows(rowB, "rowb", 5)
            for nb in range(4):
                wb, wkey = WS.next()
                for ti in range(4):
                    pi = P.next_ps()
                    for kc in range(16):
                        P.op("pe", lambda e, kc=kc, ti=ti, pi=pi, wb=wb: e.matmul(
                            ps[pi][:], lhsT=xg[:, kc, ti * 128:(ti + 1) * 128], rhs=wb[:, kc, :],
                            start=(kc == 0), stop=(kc == 15)), reads=["xg%d" % ti, wkey], writes=["ps%d" % pi])
                    P.op("act", lambda e, ti=ti, nb=nb, pi=pi: e.copy(out=VS[:, ti, nb * 512:(nb + 1) * 512], in_=ps[pi][:]),
                         reads=["ps%d" % pi], gwrites=["VS%d" % ti])
                WS.done()
            if gi == 0:
                dump("vpre", VS[:].rearrange("p a b -> p (a b)"), [128, 8192], ["VS0", "VS1", "VS2", "VS3"])
            def vln_chain(ti):
                p = ti % 2
                vk, ksm, kst, kv = "VS%d" % ti, "smallV%d" % p, "statsV%d" % p, "vln%d" % p
                c0 = 24 + 8 * p
                vl = vln2[p]
                for c in range(4):
                    P.op("dve", lambda e, c=c: e.bn_stats(out=stats[:, 8 + 4 * p + c, :], in_=VS[:, ti, c * 512:(c + 1) * 512]),
                         reads=[vk], writes=[kst])
                yield
                P.op("dve", lambda e: e.bn_aggr(out=small[:, c0:c0 + 2], in_=stats[:, 8 + 4 * p:12 + 4 * p, :].rearrange("p a b -> p (a b)")),
                     reads=[kst], writes=[ksm])
                yield
                P.op("act", lambda e: e.activation(out=small[:, c0 + 2:c0 + 3], in_=small[:, c0 + 1:c0 + 2], func=AF.Sqrt, bias=EPS), writes=[ksm])
                yield
                P.op("dve", lambda e: e.reciprocal(out=small[:, c0 + 3:c0 + 4], in_=small[:, c0 + 2:c0 + 3]), writes=[ksm])
                yield
                P.op("dve", lambda e: e.scalar_tensor_tensor(out=small[:, c0 + 4:c0 + 5], in0=small[:, c0:c0 + 1], scalar=-1.0,
                                                             in1=small[:, c0 + 3:c0 + 4], op0=ALU.mult, op1=ALU.mult), writes=[ksm])
                yield
                P.op("act", lambda e: e.activation(out=VS[:, ti, :], in_=VS[:, ti, :], func=AF.Identity, scale=small[:, c0 + 3:c0 + 4],
                                                   bias=small[:, c0 + 4:c0 + 5]), reads=[ksm], writes=[vk])
                yield
                P.op("dve", lambda e: e.tensor_tensor(out=VS[:, ti, :], in0=VS[:, ti, :], in1=rowA[:], op=ALU.mult),
                     reads=["rowg"], writes=[vk])
                yield
                P.op("dve", lambda e: e.tensor_tensor(out=vl[:], in0=VS[:, ti, :], in1=rowB[:], op=ALU.add),
                     reads=[vk, "rowb"], writes=[kv])
                yield
                for hp in range(4):
                    pi = P.next_ps()
                    for hh in range(2):
                        h = 2 * hp + hh
                        P.op("pe", lambda e, h=h, hh=hh, pi=pi: e.matmul(
                            ps[pi][:, hh * 256:(hh + 1) * 256], lhsT=wspT[:, h, :], rhs=vl[:, h * 256:(h + 1) * 256],
                            start=True, stop=True), reads=["wspT", kv], writes=["ps%d" % pi])
                    yield
                    for hh in range(2):
                        h = 2 * hp + hh
                        P.op("act" if hh == 0 else "dve",
                             (lambda e, h=h, hh=hh, pi=pi: e.activation(
                                 out=VS[:, ti, h * 256:(h + 1) * 256], in_=ps[pi][:, hh * 256:(hh + 1) * 256],
                                 func=AF.Identity, bias=bsp[:, h:h + 1])) if hh == 0 else
                             (lambda e, h=h, hh=hh, pi=pi: e.tensor_scalar(
                                 out=VS[:, ti, h * 256:(h + 1) * 256], in0=ps[pi][:, hh * 256:(hh + 1) * 256],
                                 scalar1=bsp[:, h:h + 1], scalar2=None, op0=ALU.add)),
                             reads=["ps%d" % pi, "bsp"], gwrites=[vk])
                    yield

            run_il([vln_chain(ti) for ti in range(4)], width=2)
            if gi == 0:
                dump("s", VS[:].rearrange("p a b -> p (a b)"), [128, 8192], ["VS0", "VS1", "VS2", "VS3"])
            for nb in range(4):
                wu, ukey = WS.next()
                pus = []
                for ti in range(4):
                    pu = P.next_ps()
                    pus.append(pu)
                    for kc in range(16):
                        P.op("pe", lambda e, kc=kc, ti=ti, pu=pu, wu=wu: e.matmul(
                            ps[pu][:], lhsT=xg[:, kc, ti * 128:(ti + 1) * 128], rhs=wu[:, kc, :],
                            start=(kc == 0), stop=(kc == 15)), reads=["xg%d" % ti, ukey], writes=["ps%d" % pu])
                WS.done()
                wz, zkey = WS.next()
                for ti in range(4):
                    vk = "VS%d" % ti
                    pu = pus[ti]
                    pz = P.next_ps()
                    for kc in range(16):
                        P.op("pe", lambda e, kc=kc, ti=ti, pz=pz, wz=wz: e.matmul(
                            ps[pz][:], lhsT=xg[:, kc, ti * 128:(ti + 1) * 128], rhs=wz[:, kc, :],
                            start=(kc == 0), stop=(kc == 15)), reads=["xg%d" % ti, zkey], writes=["ps%d" % pz])
                    t5, t5k = tmp512[ti % 2], "tmp512_%d" % (ti % 2)
                    P.op("act", lambda e, pz=pz, t5=t5: e.activation(out=t5[:], in_=ps[pz][:], func=AF.Silu),
                         reads=["ps%d" % pz], writes=[t5k])
                    P.op("dve", lambda e, ti=ti, nb=nb, pu=pu: e.tensor_tensor(
                        out=VS[:, ti, nb * 512:(nb + 1) * 512], in0=ps[pu][:], in1=VS[:, ti, nb * 512:(nb + 1) * 512],
                        op=ALU.mult), reads=["ps%d" % pu], writes=[vk])
                    P.op("dve", lambda e, ti=ti, nb=nb, t5=t5: e.tensor_tensor(
                        out=VS[:, ti, nb * 512:(nb + 1) * 512], in0=VS[:, ti, nb * 512:(nb + 1) * 512], in1=t5[:],
                        op=ALU.mult), reads=[t5k], writes=[vk])
                WS.done()
            if gi == 0:
                dump("y", VS[:].rearrange("p a b -> p (a b)"), [128, 8192], ["VS0", "VS1", "VS2", "VS3"])
            for ti in range(4):
                for q4 in range(4):
                    pi = P.next_ps()
                    for j4 in range(4):
                        kc = q4 * 4 + j4
                        P.op("pe", lambda e, kc=kc, j4=j4, pi=pi, ti=ti: e.transpose(
                            out=ps[pi][:, j4 * 128:(j4 + 1) * 128], in_=VS[:, ti, kc * 128:(kc + 1) * 128], identity=ident_f[:]),
                            reads=["VS%d" % ti, "ident_f"], writes=["ps%d" % pi])
                    P.op("act" if q4 % 2 == 0 else "dve",
                         (lambda e, q4=q4, pi=pi, ti=ti: e.copy(
                             out=xg[:, q4 * 4:(q4 + 1) * 4, ti * 128:(ti + 1) * 128],
                             in_=ps[pi][:].rearrange("p (a b) -> p a b", a=4))) if q4 % 2 == 0 else
                         (lambda e, q4=q4, pi=pi, ti=ti: e.tensor_copy(
                             out=xg[:, q4 * 4:(q4 + 1) * 4, ti * 128:(ti + 1) * 128],
                             in_=ps[pi][:].rearrange("p (a b) -> p a b", a=4))),
                         reads=["ps%d" % pi], gwrites=["xg%d" % ti])
            if gi == 0:
                dump("pre", XR[:, 0:4, :].rearrange("p a b -> p (a b)"), [128, 8192], ["XR0", "XR1", "XR2", "XR3"])
            load_rows(rowA, "rowg", 2)
            load_rows(rowB, "rowb", 3)
            out_proj(WS, xg, "xg%d", tiles, g_row, tmp512,
                     after_tile=lambda t: layer_norm_tile(t, rowA, rowB, None, store=True))
    elif L0:
        for t in range(8):
            P.dma("sp", "st%d" % (t % 2), lambda e, t=t: e.dma_start(out=out_d[t * 128:(t + 1) * 128, :], in_=XR[:, t, :]),
                  reads=["XR%d" % t], final=True)

    P.finalize()
    return nc


def _blk(w):
    return np.ascontiguousarray(w.reshape(16, 128, 512).transpose(1, 0, 2)).reshape(128, 8192)


def _prep_shared(inp):
    f = np.float32
    sh = {}
    sh["ident"] = np.eye(128, dtype=f)
    aw = inp["ada_w"]
    sh["adaw"] = np.ascontiguousarray(
        aw.reshape(2, 16, 128, 12, 512).transpose(0, 3, 2, 1, 4)).reshape(24, 128, 8192)
    ab = inp["ada_b"].reshape(2, 48, 128).transpose(2, 0, 1)
    sh["adabf"] = np.ascontiguousarray(np.repeat(ab[:, :, :, None], 2, axis=3)).reshape(128, 192)
    rows = np.zeros((10, 2048), f)
    rows[0] = inp["post_ln_g"][0]
    rows[1] = inp["post_ln_b"][0]
    rows[2] = inp["post_ln_g"][1]
    rows[3] = inp["post_ln_b"][1]
    rows[4] = inp["sgu_ln_g"][0]
    rows[5] = inp["sgu_ln_b"][0]
    rows[6, :1024] = inp["mh_norm_g"][0]
    rows[6, 1024:] = inp["pool_scale"][0]
    sh["rowsb"] = np.ascontiguousarray(np.broadcast_to(rows[:, None, :], (10, 128, 2048)))
    we = inp["w_in_even"][0]
    cols = list(range(0, 5120)) + list(range(5136, 7184))
    wes = we[:, cols]
    blocks = [_blk(wes[:, i * 512:(i + 1) * 512]) for i in range(14)]
    blocks += [_blk(inp["w_out_even"][0][:, i * 512:(i + 1) * 512]) for i in range(4)]
    wo = inp["w_in_odd"][0]
    blocks += [_blk(wo[:, i * 512:(i + 1) * 512]) for i in range(12)]
    blocks += [_blk(inp["w_out_odd"][0][:, i * 512:(i + 1) * 512]) for i in range(4)]
    sh["wst"] = np.stack(blocks)
    return sh


def _prep_core(inp, sh, b, half, layers, x_override=None):
    f = np.float32
    flip = half == 1
    m = dict(sh)
    xsrc = inp["x"] if x_override is None else x_override
    xb = xsrc[b][::-1] if flip else xsrc[b]
    m["xo"] = np.ascontiguousarray(xb[:1024])
    cc = np.stack([inp["c"][b], inp["c_ctx"]], axis=-1)
    m["cc"] = np.ascontiguousarray(cc.reshape(16, 128, 2).transpose(1, 0, 2)).reshape(128, 32)
    if 0 in layers:
        cb = inp["ctx"][b][::-1] if flip else inp["ctx"][b]

        def tiles_T(a):
            n = a.shape[0] // 128
            return np.ascontiguousarray(a.reshape(n, 128, 16, 128).transpose(0, 3, 2, 1)).reshape(n, 128, 2048)
        m["xoT"] = tiles_T(xb[:1024])
        m["xothT"] = tiles_T(xb[1024:])
        m["xcT"] = tiles_T(cb)
        m.update(_prep_l0_consts(inp, flip))
    if 1 in layers:
        wsp = inp["w_sp"][0]
        bsp = inp["b_sp"][0]
        if flip:
            wsp = wsp[:, ::-1, ::-1]
            bsp = bsp[:, ::-1]
        m["wspT"] = np.ascontiguousarray(wsp.transpose(2, 0, 1)).reshape(128, 1024)
        m["bsp"] = np.ascontiguousarray(bsp.T)
    return m


def _prep_l0_consts(inp, flip):
    f = np.float32
    m = {}
    we = inp["w_in_even"][0]
    gcols = []
    for i_f in range(2):
        for ld in range(2):
            gd = (1 - ld) if flip else ld
            for h in range(4):
                gcols.append(5120 + (i_f * 2 + gd) * 4 + h)
    wgm = we[:, gcols]
    m["wg"] = np.ascontiguousarray(wgm.reshape(16, 128, 16).transpose(1, 0, 2)).reshape(128, 256)
    bi, bf = inp["b_igate"][0], inp["b_fgate"][0]
    if flip:
        bi, bf = bi[::-1], bf[::-1]
    gb = np.concatenate([bi.reshape(-1), bf.reshape(-1)]).astype(f)
    m["gbias"] = np.ascontiguousarray(np.broadcast_to(gb[None, :], (128, 16)))
    cw = inp["conv_qk"][0]
    if flip:
        cw = cw[::-1]
    m["convw"] = np.ascontiguousarray(cw.T.reshape(16, 128, 3).transpose(1, 0, 2)).reshape(128, 48)
    p = np.arange(128)
    le = (p[:, None] <= p[None, :]).astype(f)
    ge = (p[:, None] >= p[None, :]).astype(f)
    m["tri"] = np.stack([le, ge, le, ge])
    pm = np.zeros((4, 128, 128), f)
    pos = np.arange(64)
    for g, w in enumerate((2, 4, 8, 16)):
        lo = np.maximum(pos - w // 2, 0)
        hi = np.minimum(pos + (w - 1 - w // 2), 63)
        M = np.zeros((64, 64), f)
        for t in range(64):
            M[t, lo[t]:hi[t] + 1] = f(1.0) / f(hi[t] - lo[t] + 1)
        M -= np.eye(64, dtype=f)
        pm[g, :64, :64] = M
        pm[g, 64:, 64:] = M
        if flip:
            pm[g] = pm[g][::-1, ::-1]
    m["pmT"] = np.ascontiguousarray(pm.transpose(0, 2, 1))
    pw = inp["pool_w"][0]
    m["poolw"] = np.ascontiguousarray(pw.reshape(4, 2, 128, 256).transpose(2, 0, 1, 3)).reshape(128, 2048)
    return m


_CACHE = {}


def _get_nc(layers):
    key = tuple(layers)
    if key not in _CACHE:
        _CACHE[key] = build(list(layers))
    return _CACHE[key]


def _run(inp, layers, x_override=None):
    sh = _prep_shared(inp)
    in_maps = []
    for r in range(8):
        b, half = r // 2, r % 2
        m = _prep_core(inp, sh, b, half, layers, x_override)
        if tuple(layers) == (0,):
            m["wst"] = sh["wst"][:NBLK0]
            m["adaw"] = sh["adaw"][:12]
            m["adabf"] = np.ascontiguousarray(sh["adabf"][:, :96])
        elif tuple(layers) == (1,):
            m["wst"] = sh["wst"][NBLK0:]
            m["adaw"] = sh["adaw"][12:]
            m["adabf"] = np.ascontiguousarray(sh["adabf"][:, 96:])
        in_maps.append(m)
    nc = _get_nc(layers)
    names = None
    res = run_bass_kernel_spmd(nc, in_maps, core_ids=list(range(8)))
    global LAST_RES
    LAST_RES = res.results
    out = np.zeros((4, 2048, 2048), np.float32)
    for r in range(8):
        b, half = r // 2, r % 2
        o = res.results[r]["out"]
        if half == 1:
            out[b, 1024:] = o[::-1]
        else:
            out[b, :1024] = o
    return out


def kernel(**inputs):
    inp = {k: np.asarray(v) for k, v in inputs.items()}
    return _run(inp, (0, 1))
```

```python
import numpy as np
import concourse.bass as bass
import concourse.mybir as mybir
from concourse.bass_utils import run_bass_kernel_spmd

F32 = mybir.dt.float32
BF16 = mybir.dt.bfloat16
U8 = mybir.dt.uint8
ALU = mybir.AluOpType
AF = mybir.ActivationFunctionType

COMPUTE = ("pe", "act", "dve", "pool")
ALPHA = 4.0 ** 0.25
EPS = 1e-5
NBLK0 = 18
NBLK1 = 16


class Prog:
    def __init__(self, nc):
        self.nc = nc
        self.streams = {e: [] for e in COMPUTE + ("sp",)}
        self.count = {e: 0 for e in COMPUTE}
        self.last_w = {}
        self.readers = {}
        self.dma_count = {}
        self.final_tokens = []
        self.barrier_deps = {}
        self.groups = {}
        self.ps_rr = 0

    def next_ps(self):
        i = self.ps_rr
        self.ps_rr = (i + 1) % 8
        return i

    def _deps(self, reads, writes, stream, gwrites=()):
        deps = set()
        for k in reads:
            if k in self.last_w:
                deps.add(self.last_w[k])
            for g in self.groups.get(k, ()):
                deps.add(g)
        for k in gwrites:
            if k in self.last_w:
                deps.add(self.last_w[k])
            for r in self.readers.get(k, ()):
                deps.add(r)
        for k in writes:
            if k in self.last_w:
                deps.add(self.last_w[k])
            for r in self.readers.get(k, ()):
                deps.add(r)
            for g in self.groups.get(k, ()):
                deps.add(g)
        if stream in self.barrier_deps:
            deps |= self.barrier_deps.pop(stream)
        return deps

    def _commit(self, token, reads, writes, gwrites=()):
        for k in reads:
            self.readers.setdefault(k, []).append(token)
        for k in writes:
            self.last_w[k] = token
            self.readers[k] = []
            self.groups.pop(k, None)
        for k in gwrites:
            self.groups.setdefault(k, []).append(token)

    @staticmethod
    def _split(reads, writes):
        r2 = [k for k in reads if not k.startswith("ps")]
        w2 = list(writes) + [k for k in reads if k.startswith("ps")]
        return r2, w2

    def op(self, eng, fn, reads=(), writes=(), gwrites=()):
        reads, writes = self._split(reads, writes)
        deps = self._deps(reads, writes, eng, gwrites)
        self.count[eng] += 1
        token = ("e:" + eng, self.count[eng])
        self.streams[eng].append((deps, fn, token))
        self._commit(token, reads, writes, gwrites)
        return token

    def dma(self, queue, slot, fn, reads=(), writes=(), final=False):
        deps = self._deps(reads, writes, queue)
        self.dma_count[slot] = self.dma_count.get(slot, 0) + 16
        token = ("d:" + slot, self.dma_count[slot])
        self.streams[queue].append((deps, fn, token))
        self._commit(token, reads, writes)
        if final:
            self.final_tokens.append(token)
        return token

    def barrier(self):
        self.marks = getattr(self, "marks", []) + [dict(self.count)]
        toks = set()
        for e in COMPUTE:
            if self.count[e]:
                toks.add(("e:" + e, self.count[e]))
        for s, v in self.dma_count.items():
            toks.add(("d:" + s, v))
        for s in self.streams:
            self.barrier_deps[s] = set(toks) | self.barrier_deps.get(s, set())

    def finalize(self):
        from contextlib import ExitStack
        nc = self.nc
        with ExitStack() as es:
            sems = {}
            for e in COMPUTE:
                sems["e:" + e] = es.enter_context(nc.semaphore("sem_" + e))
            for s in self.dma_count:
                sems["d:" + s] = es.enter_context(nc.semaphore("dsem_" + s))
            block = es.enter_context(nc.Block())
            final_tokens = list(self.final_tokens)
            streams = self.streams

            need = {e: set() for e in COMPUTE}
            for sn in streams:
                for deps, fn, token in streams[sn]:
                    mx = {}
                    for (sname, val) in deps:
                        mx[sname] = max(mx.get(sname, 0), val)
                    for sname, val in mx.items():
                        if sname.startswith("e:"):
                            need[sname[2:]].add(val)
            remap = {}
            for e in COMPUTE:
                if self.count[e]:
                    need[e].add(self.count[e])
                remap[e] = {v: i + 1 for i, v in enumerate(sorted(need[e]))}

            def emit(stream_name, eng):
                waited = {}
                for deps, fn, token in streams[stream_name]:
                    mx = {}
                    for (sname, val) in deps:
                        mx[sname] = max(mx.get(sname, 0), val)
                    for sname in sorted(mx):
                        val = mx[sname]
                        if sname == "e:pe" and stream_name == "pe":
                            continue
                        if sname.startswith("e:"):
                            val = remap[sname[2:]][val]
                        if waited.get(sname, 0) >= val:
                            continue
                        eng.wait_ge(sems[sname], val)
                        waited[sname] = val
                    inst = fn(eng)
                    if token[0].startswith("e:"):
                        if token[1] in need[token[0][2:]]:
                            inst.then_inc(sems[token[0]], 1)
                    else:
                        inst.then_inc(sems[token[0]], 16)
                if stream_name == "sp":
                    fin = {}
                    for (sname, val) in final_tokens:
                        fin[sname] = max(fin.get(sname, 0), val)
                    for (sname, val) in sorted(fin.items()):
                        if waited.get(sname, 0) < val:
                            eng.wait_ge(sems[sname], val)
                            waited[sname] = val

            @block.sync
            def _(e):
                emit("sp", e)

            @block.tensor
            def _(e):
                emit("pe", e)

            @block.scalar
            def _(e):
                emit("act", e)

            @block.vector
            def _(e):
                emit("dve", e)

            @block.gpsimd
            def _(e):
                emit("pool", e)


class Arena:
    def __init__(self, nc, nbytes):
        self.nc = nc
        self.cm = nc.sbuf_tensor("arena", [128, nbytes], U8)
        self.cm.__enter__()
        self.base = list(nc.allocations)[-1].memorylocations[0].addr
        self.size = nbytes
        self.top = 0
        self.n = 0

    def alloc(self, shape, dtype, at=None):
        nb = int(np.prod(shape[1:])) * (4 if dtype == F32 else 2)
        nb = (nb + 63) // 64 * 64
        if at is None:
            off = self.top
            self.top += nb
        else:
            off = at
        assert off + nb <= self.size, ("SBUF arena overflow", off, nb, self.size)
        self.n += 1
        return self.nc.alloc_sbuf_tensor_at("t%d" % self.n, list(shape), dtype, offset=self.base + off)


DEBUG = []


def build(layers):
    nc = bass.Bass("TRN2", target_bir_lowering=False)
    P = Prog(nc)
    dbg_n = [0]

    def dump(name, ap, shape, keys):
        if name not in DEBUG:
            return
        d = nc.dram_tensor("dbg_" + name, list(shape), F32, kind="ExternalOutput").ap()
        dbg_n[0] += 1
        q = "pool" if ap.dtype != F32 else "sp"
        P.dma(q, "dbg%d" % dbg_n[0], lambda e: e.dma_start(out=d, in_=ap), reads=keys, final=True)
    L0 = 0 in layers
    L1 = 1 in layers

    def din(name, shape):
        return nc.dram_tensor(name, list(shape), F32, kind="ExternalInput").ap()

    xo = din("xo", [1024, 2048])
    cc_d = din("cc", [128, 32])
    adaw_d = din("adaw", [12 * len(layers), 128, 8192])
    adabf_d = din("adabf", [128, len(layers) * 96])
    ident_d = din("ident", [128, 128])
    rows_d = din("rowsb", [10, 128, 2048])
    WB0 = 0
    WB1 = NBLK0 if L0 else 0
    AD1 = 24 if L0 else 0
    wst_d = din("wst", [(NBLK0 if L0 else 0) + (NBLK1 if L1 else 0), 128, 8192])
    if L0:
        xoT_d = din("xoT", [8, 128, 2048])
        xothT_d = din("xothT", [8, 128, 2048])
        xcT_d = din("xcT", [2, 128, 2048])
        wg_d = din("wg", [128, 256])
        gb_d = din("gbias", [128, 16])
        convw_d = din("convw", [128, 48])
        tri_d = din("tri", [4, 128, 128])
        pm_d = din("pmT", [4, 128, 128])
        poolw_d = din("poolw", [128, 2048])
    if L1:
        wsp_d = din("wspT", [128, 1024])
        bsp_d = din("bsp", [128, 8])
    out_d = nc.dram_tensor("out", [1024, 2048], F32, kind="ExternalOutput").ap()

    A = Arena(nc, 212800)
    ps = []
    for i in range(8):
        cm = nc.psum_tensor("psb%d" % i, [128, 512], F32)
        ps.append(cm.__enter__())

    ident_f = A.alloc([128, 128], F32)
    ident_b = A.alloc([128, 128], BF16)
    ones_f = A.alloc([128, 128], F32)
    cc = A.alloc([128, 16, 2], F32)
    csil = A.alloc([128, 16, 2], F32)
    adabf = A.alloc([128, len(layers), 96], F32)
    mods = {l: A.alloc([128, 48, 2], F32) for l in layers}
    sc1ps = {l: A.alloc([128, 16, 2], F32) for l in layers}
    mod = mods[layers[0]]
    small = A.alloc([128, 64], F32)
    stats = A.alloc([128, 16, 6], F32)
    XR_OFF = A.size - 65536
    XR = A.alloc([128, 8, 2048], F32, at=XR_OFF)
    P.dma("sp", "c0", lambda e: e.dma_start(out=ident_f[:], in_=ident_d[:, :]), writes=["ident_f"])
    P.dma("sp", "c1", lambda e: e.dma_start(out=cc[:].rearrange("p a b -> p (a b)"), in_=cc_d[:, :]), writes=["cc"])
    P.dma("sp", "c2", lambda e: e.dma_start(out=adabf[:].rearrange("p a b -> p (a b)"), in_=adabf_d[:, :]), writes=["adabf"])
    P.op("dve", lambda e: e.tensor_copy(out=ident_b[:], in_=ident_f[:]), reads=["ident_f"], writes=["ident_b"])
    P.op("dve", lambda e: e.memset(ones_f[:], 1.0), writes=["ones_f"])
    P.op("act", lambda e: e.activation(out=csil[:], in_=cc[:], func=AF.Silu), reads=["cc"], writes=["csil"])
    base_top = A.top

    def wload(wb, key, blk, c0=0, ncols=512, d0=None):
        d0 = c0 if d0 is None else d0
        src = wst_d[blk].rearrange("p (k n) -> p k n", k=16)[:, :, c0:c0 + ncols]
        P.dma("pool", "d" + key, lambda e: e.dma_start(out=wb[:, :, d0:d0 + ncols], in_=src), writes=[key])

    def run_il(gens, width=2, bg=None):
        gens = list(gens)
        bg = list(bg) if bg else []
        active = []
        cur_bg = None
        while gens or active or bg or cur_bg is not None:
            while gens and len(active) < width:
                active.append(gens.pop(0))
            for g_ in list(active):
                try:
                    next(g_)
                except StopIteration:
                    active.remove(g_)
            if cur_bg is None and bg:
                cur_bg = bg.pop(0)
            if cur_bg is not None:
                try:
                    next(cur_bg)
                except StopIteration:
                    cur_bg = None

    class WStream:
        def __init__(self, bufs, seq):
            self.bufs, self.seq = bufs, seq
            self.issued = self.consumed = self.released = 0
            self.done()

        def _issue(self):
            wb, key = self.bufs[self.issued % len(self.bufs)]
            wload(wb, key, *self.seq[self.issued])
            self.issued += 1

        def next(self):
            if self.consumed == self.issued:
                assert self.issued - self.released < len(self.bufs)
                self._issue()
            wb, key = self.bufs[self.consumed % len(self.bufs)]
            self.consumed += 1
            return wb, key

        def done(self):
            self.released = self.consumed
            while self.issued < len(self.seq) and self.issued - self.released < len(self.bufs):
                self._issue()

    cur = {}

    ada_t = {}

    def ada_setup():
        m0 = A.top
        A.top = 134 * 1024
        ada_t["abufs"] = [(A.alloc([128, 16, 512], BF16), "ab%d" % i) for i in range(3)]
        ada_t["csil_b"] = A.alloc([128, 16, 2], BF16)
        ada_t["modrow"] = A.alloc([2, 6144], F32)
        ada_t["n"] = 0
        A.top = m0
        csil_b = ada_t["csil_b"]
        P.op("dve", lambda e: e.tensor_copy(out=csil_b[:], in_=csil[:]), reads=["csil"], writes=["csil_b"])

    def ada_gen(li, layer):
        abufs, csil_b, modrow = ada_t["abufs"], ada_t["csil_b"], ada_t["modrow"]
        mk = "modrow%d" % li
        for blk in range(12):
            ab, key = abufs[ada_t["n"] % 3]
            ada_t["n"] += 1
            P.dma("pool", "d" + key, lambda e, ab=ab, blk=blk: e.dma_start(
                out=ab[:].rearrange("p k n -> p (k n)"), in_=adaw_d[li * 12 + blk]), writes=[key])
            yield
            pi = P.next_ps()
            for kc in range(16):
                P.op("pe", lambda e, ab=ab, kc=kc, pi=pi: e.matmul(
                    ps[pi][0:2, :], lhsT=csil_b[:, kc, :], rhs=ab[:, kc, :], start=(kc == 0), stop=(kc == 15)),
                    reads=[key, "csil_b"], writes=["ps%d" % pi])
            yield
            P.op("act", lambda e, blk=blk, pi=pi: e.copy(out=modrow[:, blk * 512:(blk + 1) * 512], in_=ps[pi][0:2, :]),
                 reads=["ps%d" % pi], writes=[mk])
            yield
        pi = P.next_ps()
        for c in range(48):
            P.op("pe", lambda e, c=c, pi=pi: e.transpose(out=ps[pi][:, 2 * c:2 * c + 2], in_=modrow[0:2, c * 128:(c + 1) * 128],
                                                         identity=ident_f[0:2, 0:2]),
                 reads=[mk, "ident_f"], writes=["ps%d" % pi])
        yield
        mt, st_ = mods[layer], sc1ps[layer]
        P.op("dve", lambda e, pi=pi: e.tensor_tensor(out=mt[:].rearrange("p a b -> p (a b)"), in0=ps[pi][:, 0:96],
                                                      in1=adabf[:, li, :], op=ALU.add),
             reads=["ps%d" % pi, "adabf"], writes=["mod%d" % layer])
        yield
        P.op("dve", lambda e: e.tensor_scalar(out=st_[:], in0=mt[:, 16:32, :], scalar1=1.0, scalar2=None, op0=ALU.add),
             reads=["mod%d" % layer], writes=["sc1p%d" % layer])
        yield

    def ada(layer, g_row):
        mod_t = mods[layer]
        cur["mod"], cur["sc1p"], cur["layer"] = mod_t, sc1ps[layer], layer
        m0 = A.top
        diag = A.alloc([128, 128], F32)
        for q4 in range(4):
            pi = P.next_ps()
            for j4 in range(4):
                j = q4 * 4 + j4
                P.op("dve", lambda e, j=j: e.tensor_scalar(out=diag[:], in0=ident_f[:], scalar1=mod_t[:, 32 + j, 0:1],
                                                            scalar2=None, op0=ALU.mult),
                     reads=["ident_f", "mod%d" % layer], writes=["diag"])
                P.op("pe", lambda e, j4=j4, pi=pi: e.matmul(ps[pi][:, j4 * 128:(j4 + 1) * 128], lhsT=ones_f[:], rhs=diag[:],
                                                            start=True, stop=True),
                     reads=["ones_f", "diag"], writes=["ps%d" % pi])
            P.op("act", lambda e, q4=q4, pi=pi: e.copy(out=g_row[:, q4 * 512:(q4 + 1) * 512], in_=ps[pi][:]),
                 reads=["ps%d" % pi], writes=["g_row"])
        P.barrier()
        A.top = m0

    def make_xmT(src, src_key, dst, dst_key, col0, which, keep=None):
        lo, hi = keep if keep is not None else (0, 128)
        mod, sc1p = cur["mod"], cur["sc1p"]
        mkey, skey = "mod%d" % cur["layer"], "sc1p%d" % cur["layer"]
        for q4 in range(4):
            pi = P.next_ps()
            for j4 in range(4):
                kc = q4 * 4 + j4
                P.op("pe", lambda e, kc=kc, j4=j4, pi=pi: e.transpose(
                    out=ps[pi][:, j4 * 128:(j4 + 1) * 128], in_=src[:, kc * 128:(kc + 1) * 128], identity=ident_f[:]),
                    reads=[src_key, "ident_f"], writes=["ps%d" % pi])
            for j4 in range(4):
                kc = q4 * 4 + j4
                eng = "act" if j4 % 2 == 0 else "dve"
                o_ap = dst[:, kc, col0:col0 + (hi - lo)]
                i_ap = ps[pi][:, j4 * 128 + lo:j4 * 128 + hi]
                if eng == "act":
                    P.op("act", lambda e, o_ap=o_ap, i_ap=i_ap, kc=kc: e.activation(
                        out=o_ap, in_=i_ap, func=AF.Identity, scale=sc1p[:, kc, which:which + 1],
                        bias=mod[:, kc, which:which + 1]), reads=["ps%d" % pi, skey, mkey], gwrites=[dst_key])
                else:
                    P.op("dve", lambda e, o_ap=o_ap, i_ap=i_ap, kc=kc: e.tensor_scalar(
                        out=o_ap, in0=i_ap, scalar1=sc1p[:, kc, which:which + 1], scalar2=mod[:, kc, which:which + 1],
                        op0=ALU.mult, op1=ALU.add), reads=["ps%d" % pi, skey, mkey], gwrites=[dst_key])

    def make_xmT_pre(src, src_key, dst, dst_key, col0, which, keep=None, extra_key=None):
        lo, hi = keep if keep is not None else (0, 128)
        mod, sc1p = cur["mod"], cur["sc1p"]
        mkey, skey = "mod%d" % cur["layer"], "sc1p%d" % cur["layer"]
        for kc in range(16):
            o_ap = dst[:, kc, col0:col0 + (hi - lo)]
            i_ap = src[:, kc, lo:hi]
            if kc % 3 == 0:
                P.op("act", lambda e, o_ap=o_ap, i_ap=i_ap, kc=kc: e.activation(
                    out=o_ap, in_=i_ap, func=AF.Identity, scale=sc1p[:, kc, which:which + 1],
                    bias=mod[:, kc, which:which + 1]), reads=[src_key, skey, mkey], gwrites=[dst_key] + ([extra_key] if extra_key else []))
            else:
                P.op("dve", lambda e, o_ap=o_ap, i_ap=i_ap, kc=kc: e.tensor_scalar(
                    out=o_ap, in0=i_ap, scalar1=sc1p[:, kc, which:which + 1], scalar2=mod[:, kc, which:which + 1],
                    op0=ALU.mult, op1=ALU.add), reads=[src_key, skey, mkey], gwrites=[dst_key] + ([extra_key] if extra_key else []))

    def load_rows(rb, key, r):
        P.dma("sp", "d" + key, lambda e: e.dma_start(out=rb[:], in_=rows_d[r]), writes=[key])

    def layer_norm_tile(t, grow_g, grow_b, tmp, store):
        xk = "XR%d" % t
        for c in range(4):
            P.op("dve", lambda e, c=c: e.bn_stats(out=stats[:, c, :], in_=XR[:, t, c * 512:(c + 1) * 512]),
                 reads=[xk], writes=["stats"])
        P.op("dve", lambda e: e.bn_aggr(out=small[:, 0:2], in_=stats[:, 0:4, :].rearrange("p a b -> p (a b)")),
             reads=["stats"], writes=["small"])
        P.op("act", lambda e: e.activation(out=small[:, 2:3], in_=small[:, 1:2], func=AF.Sqrt, bias=EPS),
             reads=["small"], writes=["small"])
        P.op("dve", lambda e: e.reciprocal(out=small[:, 3:4], in_=small[:, 2:3]), reads=["small"], writes=["small"])
        P.op("dve", lambda e: e.scalar_tensor_tensor(out=small[:, 4:5], in0=small[:, 0:1], scalar=-1.0, in1=small[:, 3:4],
                                                     op0=ALU.mult, op1=ALU.mult), reads=["small"], writes=["small"])
        P.op("act", lambda e: e.activation(out=XR[:, t, :], in_=XR[:, t, :], func=AF.Identity, scale=small[:, 3:4],
                                           bias=small[:, 4:5]), reads=["small"], writes=[xk])
        P.op("dve", lambda e: e.tensor_tensor(out=XR[:, t, :], in0=XR[:, t, :], in1=grow_g[:], op=ALU.mult),
             reads=["rowg"], writes=[xk])
        P.op("dve", lambda e: e.tensor_tensor(out=XR[:, t, :], in0=XR[:, t, :], in1=grow_b[:], op=ALU.add),
             reads=["rowb"], writes=[xk])
        if store:
            P.dma("sp", "st%d" % (t % 2), lambda e: e.dma_start(out=out_d[t * 128:(t + 1) * 128, :], in_=XR[:, t, :]),
                  reads=[xk], final=True)

    def out_proj(WS, yT, ykey, tiles, g_row, tmp512, after_tile=None):
        def unit(nb, ti, t, wb, wkey):
            pi = P.next_ps()
            for kc in range(16):
                P.op("pe", lambda e, kc=kc: e.matmul(
                    ps[pi][:], lhsT=yT[:, kc, ti * 128:(ti + 1) * 128], rhs=wb[:, kc, :],
                    start=(kc == 0), stop=(kc == 15)), reads=[ykey % ti, wkey], writes=["ps%d" % pi])
            t5, t5k = tmp512[(nb + ti) % 2], "tmp512_%d" % ((nb + ti) % 2)
            P.op("dve", lambda e: e.tensor_tensor(out=t5[:], in0=ps[pi][:], in1=g_row[:, nb * 512:(nb + 1) * 512],
                                                  op=ALU.mult), reads=["ps%d" % pi, "g_row"], writes=[t5k])
            P.op("dve", lambda e: e.scalar_tensor_tensor(
                out=XR[:, t, nb * 512:(nb + 1) * 512], in0=XR[:, t, nb * 512:(nb + 1) * 512], scalar=ALPHA,
                in1=t5[:], op0=ALU.mult, op1=ALU.add), reads=[t5k], writes=["XR%d" % t])

        for nb in range(2):
            wb, wkey = WS.next()
            for ti, t in enumerate(tiles):
                unit(nb, ti, t, wb, wkey)
            WS.done()
        w2, k2 = WS.next()
        w3, k3 = WS.next()
        for ti, t in enumerate(tiles):
            unit(2, ti, t, w2, k2)
            unit(3, ti, t, w3, k3)
            if after_tile is not None:
                after_tile(t)
        WS.done()

    if L0:
        A.top = base_top
        g_row = A.alloc([128, 2048], F32)
        mg = A.top
        wg = A.alloc([128, 16, 16], BF16)
        gbias = A.alloc([128, 16], F32)
        convw = A.alloc([128, 16, 3], F32)
        tri = A.alloc([128, 4, 128], F32)
        pmT = A.alloc([128, 4, 128], F32)
        poolw = A.alloc([128, 4, 2, 256], BF16)
        C32 = A.alloc([128, 8, 2, 257], F32)
        C16 = A.alloc([128, 8, 2, 257], BF16)
        G = A.alloc([128, 18, 16], F32)
        EB = A.alloc([128, 18, 8], F32)
        WSg = A.alloc([128, 18, 8], F32)
        DC = A.alloc([128, 18, 8], F32)
        kw = [A.alloc([128, 256], BF16) for _ in range(4)]
        P.dma("pool", "c5", lambda e: e.dma_start(out=wg[:].rearrange("p a b -> p (a b)"), in_=wg_d[:, :]), writes=["wg"])
        P.dma("sp", "c6", lambda e: e.dma_start(out=gbias[:], in_=gb_d[:, :]), writes=["gbias"])
        P.dma("sp", "c7", lambda e: e.dma_start(out=convw[:].rearrange("p a b -> p (a b)"), in_=convw_d[:, :]), writes=["convw"])
        for i in range(4):
            P.dma("sp", "c8%d" % i, lambda e, i=i: e.dma_start(out=tri[:, i, :], in_=tri_d[i]), writes=["tri"])
            P.dma("sp", "c9%d" % i, lambda e, i=i: e.dma_start(out=pmT[:, i, :], in_=pm_d[i]), writes=["pmT"])
        P.dma("pool", "c10", lambda e: e.dma_start(out=poolw[:].rearrange("p a b c -> p (a b c)"), in_=poolw_d[:, :]), writes=["poolw"])
        P.op("dve", lambda e: e.memset(C32[:].rearrange("p a b c -> p (a b c)"), 0.0), writes=["C32"])
        m1 = A.top
    ada_setup()
    run_il([ada_gen(0, layers[0])], width=1)
    if not (L0 and L1):
        for li_ in range(1, len(layers)):
            run_il([ada_gen(li_, layers[li_])], width=1)
        P.barrier()

    if not L0:
        for t in range(8):
            P.dma("sp", "xr%d" % t, lambda e, t=t: e.dma_start(out=XR[:, t, :], in_=xo[t * 128:(t + 1) * 128, :]),
                  writes=["XR%d" % t])

    if L0:
        A.top = m1
        ada(0, g_row)

        def load_x(xt_i, src, xts):
            key = "xt%d" % xt_i
            xt_t = xts[xt_i]
            P.dma("sp", "d" + key, lambda e: e.dma_start(out=xt_t[:].rearrange("p a b -> p (a b)"), in_=src), writes=[key])
            return key

        def gates_tile_g(xT, xkeys, col0, gt):
            pi = P.next_ps()
            for kc in range(16):
                P.op("pe", lambda e, kc=kc, pi=pi: e.matmul(ps[pi][:, 0:16], lhsT=xT[:, kc, col0:col0 + 128], rhs=wg[:, kc, :],
                                                            start=(kc == 0), stop=(kc == 15)),
                     reads=list(xkeys) + ["wg"], writes=["ps%d" % pi])
            yield
            gk = "G%d" % gt
            P.op("dve", lambda e, pi=pi: e.tensor_tensor(out=G[:, gt, :], in0=ps[pi][:, 0:16], in1=gbias[:], op=ALU.add),
                 reads=["ps%d" % pi, "gbias"], writes=[gk])
            yield
            P.op("act", lambda e: e.activation(out=G[:, gt, 8:16], in_=G[:, gt, 8:16], func=AF.Exp, scale=-1.0), writes=[gk])
            yield
            P.op("act", lambda e: e.activation(out=G[:, gt, 8:16], in_=G[:, gt, 8:16], func=AF.Ln, bias=1.0), writes=[gk])
            yield
            P.op("dve", lambda e: e.tensor_scalar(out=G[:, gt, 8:16], in0=G[:, gt, 8:16], scalar1=-1.0, scalar2=None, op0=ALU.mult),
                 writes=[gk])
            yield
            pj = P.next_ps()
            P.op("pe", lambda e, pj=pj: e.matmul(ps[pj][:, 0:4], lhsT=tri[:, 0, :], rhs=G[:, gt, 8:12], start=True, stop=True),
                 reads=[gk, "tri"], writes=["ps%d" % pj])
            yield
            P.op("pe", lambda e, pj=pj: e.matmul(ps[pj][:, 4:8], lhsT=tri[:, 1, :], rhs=G[:, gt, 12:16], start=True, stop=True),
                 reads=[gk, "tri"], writes=["ps%d" % pj])
            yield
            P.op("pe", lambda e, pj=pj: e.matmul(ps[pj][:, 8:16], lhsT=ones_f[:], rhs=G[:, gt, 8:16], start=True, stop=True),
                 reads=[gk, "ones_f"], writes=["ps%d" % pj])
            yield
            P.op("dve", lambda e, pj=pj: e.tensor_tensor(out=EB[:, gt, :], in0=G[:, gt, 0:8], in1=ps[pj][:, 0:8], op=ALU.subtract),
                 reads=["ps%d" % pj, gk], writes=[gk + "q"])
            yield
            P.op("dve", lambda e, pj=pj: e.tensor_tensor(out=WSg[:, gt, :], in0=EB[:, gt, :], in1=ps[pj][:, 8:16], op=ALU.add),
                 reads=["ps%d" % pj], writes=[gk + "q"])
            yield
            P.op("act", lambda e: e.activation(out=WSg[:, gt, :], in_=WSg[:, gt, :], func=AF.Exp), writes=[gk + "q"])
            yield
            P.op("act", lambda e, pj=pj: e.activation(out=DC[:, gt, :], in_=ps[pj][:, 8:16], func=AF.Exp),
                 reads=["ps%d" % pj], writes=[gk + "q"])
            yield

        def state_update_g(k_ap, kkey, v_ap, vkey, gt, d, h, kwi):
            idx = d * 4 + h
            ck = "C%d" % idx
            kwb, kwk = kw[kwi], "kw%d" % kwi
            P.op("dve", lambda e: e.tensor_scalar(out=kwb[:], in0=k_ap, scalar1=WSg[:, gt, idx:idx + 1], scalar2=None, op0=ALU.mult),
                 reads=[kkey, "G%dq" % gt], writes=[kwk])
            yield
            for kcc in range(2):
                pa = P.next_ps()
                P.op("pe", lambda e, pa=pa, kcc=kcc: e.matmul(ps[pa][:, 0:257], lhsT=kwb[:, kcc * 128:(kcc + 1) * 128], rhs=v_ap,
                                                              start=True, stop=True),
                     reads=[kwk] + ([vkey] if isinstance(vkey, str) else list(vkey)), writes=["ps%d" % pa])
                yield
                P.op("dve", lambda e, pa=pa, kcc=kcc: e.scalar_tensor_tensor(
                    out=C32[:, idx, kcc, :], in0=C32[:, idx, kcc, :], scalar=DC[:, gt, idx:idx + 1], in1=ps[pa][:, 0:257],
                    op0=ALU.mult, op1=ALU.add), reads=["ps%d" % pa, "G%dq" % gt], writes=[ck])
                yield

        def cast_state(idx):
            P.op("act", lambda e: e.copy(out=C16[:, idx, :, :], in_=C32[:, idx, :, :]), reads=["C%d" % idx], writes=["C16_%d" % idx])

        def conv3(pre, sg, n, cwi, segs, pk="pre", sk="sg"):
            for (a, b, hl, hr) in segs:
                P.op("dve", lambda e, a=a, b=b: e.tensor_scalar(out=sg[:, a:b], in0=pre[:, a:b], scalar1=convw[:, cwi, 1:2],
                                                                  scalar2=None, op0=ALU.mult), reads=[pk, "convw"], writes=[sk])
                la = a if hl else a + 1
                P.op("dve", lambda e, la=la, b=b: e.scalar_tensor_tensor(
                    out=sg[:, la:b], in0=pre[:, la - 1:b - 1], scalar=convw[:, cwi, 0:1], in1=sg[:, la:b],
                    op0=ALU.mult, op1=ALU.add), reads=[pk, "convw"], writes=[sk])
                rb = b if hr else b - 1
                P.op("dve", lambda e, a=a, rb=rb: e.scalar_tensor_tensor(
                    out=sg[:, a:rb], in0=pre[:, a + 1:rb + 1], scalar=convw[:, cwi, 2:3], in1=sg[:, a:rb],
                    op0=ALU.mult, op1=ALU.add), reads=[pk, "convw"], writes=[sk])

        def proj_fm(wb, wkey, c0, xT, xkeys, ranges, pre, pk="pre"):
            for (a, b, o) in ranges:
                pi = P.next_ps()
                for kc in range(16):
                    P.op("pe", lambda e, kc=kc, pi=pi, a=a, b=b: e.matmul(
                        ps[pi][:, 0:b - a], lhsT=wb[:, kc, c0:c0 + 128], rhs=xT[:, kc, a:b], start=(kc == 0), stop=(kc == 15)),
                        reads=list(xkeys) + [wkey], writes=["ps%d" % pi])
                P.op("act", lambda e, pi=pi, a=a, b=b, o=o: e.copy(out=pre[:, o:o + b - a], in_=ps[pi][:, 0:b - a]),
                     reads=["ps%d" % pi], writes=[pk])

        def fm_chunk_gen(wb, wkey, c0, srcs, pre, pk, sg, sk, cwi, segs, lo, hi, is_k, out_ap, okey, tr=None):
            for (xT, xkeys, ranges) in srcs:
                for (a_, b_, o_) in ranges:
                    pi = P.next_ps()
                    for kc in range(16):
                        P.op("pe", lambda e, kc=kc, pi=pi, a_=a_, b_=b_, xT=xT: e.matmul(
                            ps[pi][:, 0:b_ - a_], lhsT=wb[:, kc, c0:c0 + 128], rhs=xT[:, kc, a_:b_], start=(kc == 0), stop=(kc == 15)),
                            reads=list(xkeys) + [wkey], writes=["ps%d" % pi])
                    P.op("act", lambda e, pi=pi, a_=a_, b_=b_, o_=o_: e.copy(out=pre[:, o_:o_ + b_ - a_], in_=ps[pi][:, 0:b_ - a_]),
                         reads=["ps%d" % pi], gwrites=[pk])
                    yield
            for (a_, b_, hl, hr) in segs:
                P.op("dve", lambda e, a_=a_, b_=b_: e.tensor_scalar(out=sg[:, a_:b_], in0=pre[:, a_:b_], scalar1=convw[:, cwi, 1:2],
                                                                      scalar2=None, op0=ALU.mult), reads=[pk, "convw"], writes=[sk])
                yield
                la = a_ if hl else a_ + 1
                P.op("dve", lambda e, la=la, b_=b_: e.scalar_tensor_tensor(
                    out=sg[:, la:b_], in0=pre[:, la - 1:b_ - 1], scalar=convw[:, cwi, 0:1], in1=sg[:, la:b_],
                    op0=ALU.mult, op1=ALU.add), reads=[pk, "convw"], writes=[sk])
                yield
                rb_ = b_ if hr else b_ - 1
                P.op("dve", lambda e, a_=a_, rb_=rb_: e.scalar_tensor_tensor(
                    out=sg[:, a_:rb_], in0=pre[:, a_ + 1:rb_ + 1], scalar=convw[:, cwi, 2:3], in1=sg[:, a_:rb_],
                    op0=ALU.mult, op1=ALU.add), reads=[pk, "convw"], writes=[sk])
                yield
            if not is_k:
                P.op("act", lambda e: e.activation(out=out_ap, in_=sg[:, lo:hi], func=AF.Silu), reads=[sk], writes=[okey])
                yield
            else:
                P.op("act", lambda e: e.activation(out=pre[:, lo:hi], in_=sg[:, lo:hi], func=AF.Sigmoid), reads=[sk], writes=[pk])
                yield
                P.op("dve", lambda e: e.scalar_tensor_tensor(out=out_ap, in0=sg[:, lo:hi], scalar=0.0625, in1=pre[:, lo:hi],
                                                             op0=ALU.mult, op1=ALU.mult), reads=[sk, pk], writes=[okey])
                yield
            if tr is not None:
                src, col0, ntile, dst_fn, dkey = tr
                for j0 in range(0, ntile, 4):
                    n = min(4, ntile - j0)
                    pi = P.next_ps()
                    psb = ps[pi][:].bitcast(BF16)
                    for j in range(n):
                        P.op("pe", lambda e, j=j, j0=j0, psb=psb: e.transpose(
                            out=psb[:, j * 128:(j + 1) * 128], in_=src[:, col0 + (j0 + j) * 128:col0 + (j0 + j + 1) * 128],
                            identity=ident_b[:]), reads=[okey, "ident_b"], writes=["ps%d" % pi])
                    P.op("act", lambda e, j0=j0, n=n, psb=psb: e.copy(
                        out=dst_fn(j0, n), in_=psb[:, 0:n * 128].rearrange("p (a b) -> p a b", a=n)),
                        reads=["ps%d" % pi], gwrites=[dkey])
                    yield

        def transposes_to(src, skey, col0, ntile, dst_fn, dkey):
            for j0 in range(0, ntile, 4):
                n = min(4, ntile - j0)
                pi = P.next_ps()
                psb = ps[pi][:].bitcast(BF16)
                for j in range(n):
                    P.op("pe", lambda e, j=j, j0=j0, psb=psb: e.transpose(
                        out=psb[:, j * 128:(j + 1) * 128], in_=src[:, col0 + (j0 + j) * 128:col0 + (j0 + j + 1) * 128],
                        identity=ident_b[:]), reads=[skey, "ident_b"], writes=["ps%d" % pi])
                P.op("act", lambda e, j0=j0, n=n, psb=psb: e.copy(
                    out=dst_fn(j0, n), in_=psb[:, 0:n * 128].rearrange("p (a b) -> p a b", a=n)),
                    reads=["ps%d" % pi], gwrites=[dkey])

        xmB = A.alloc([128, 16, 1025], BF16)
        xmC = A.alloc([128, 16, 256], BF16)
        m1x = A.top
        xts = [A.alloc([128, 16, 128], F32) for _ in range(4)]
        assert A.top <= 134 * 1024
        A.top = m1x
        ktok_oc = A.alloc([128, 10, 1024], BF16)
        vext_oc = A.alloc([128, 10, 4, 257], BF16)
        pre1b = [A.alloc([128, 1281], F32) for _ in range(2)]
        sg1b = [A.alloc([128, 1281], F32) for _ in range(2)]
        kTfb = [A.alloc([128, 1281], BF16) for _ in range(2)]
        wbufs = [(A.alloc([128, 16, 512], BF16), "w%d" % i) for i in range(2)]
        fprog = {"t": 0}

        def front_gen():
            k = load_x(0, xoT_d[7], xts)
            make_xmT_pre(xts[0], k, xmB, "xmB", 0, 0, keep=(127, 128))
            yield
            for j in range(8):
                k = load_x((j + 1) % 4, xothT_d[j], xts)
                make_xmT_pre(xts[(j + 1) % 4], k, xmB, "xmB", 1 + j * 128, 0, extra_key="xmBt%d" % j)
                fprog["t"] = j + 1
                yield
            for j in range(2):
                k = load_x((j + 1) % 4, xcT_d[j], xts)
                make_xmT_pre(xts[(j + 1) % 4], k, xmC, "xmC", j * 128, 1, extra_key="xmBt%d" % (8 + j))
                fprog["t"] = 9 + j
                yield

        def gated_gates(i):
            while fprog["t"] <= i:
                yield
            if i < 8:
                yield from gates_tile_g(xmB, ["xmBt%d" % i], 1 + i * 128, 8 + i)
            else:
                yield from gates_tile_g(xmC, ["xmBt%d" % i], (i - 8) * 128, 8 + i)

        def front_all():
            gens = [front_gen()] + [gated_gates(i) for i in range(10)]
            active = []
            while gens or active:
                while gens and len(active) < 5:
                    active.append(gens.pop(0))
                for g_ in list(active):
                    try:
                        next(g_)
                    except StopIteration:
                        active.remove(g_)
                yield

        if L1:
            run_il([ada_gen(1, 1), front_all()], width=2)
        else:
            run_il([front_all()], width=1)
        P.barrier()
        P.op("dve", lambda e: e.memset(vext_oc[:, :, :, 256:257], 1.0), writes=["vext_oc"])
        WS = WStream(wbufs, [(2, 0, 512), (3, 0, 512), (4, 0, 512), (5, 0, 512)])
        wk0 = WS.next()
        wk1 = WS.next()
        rngB = [(0, 512, 0), (512, 1024, 512), (1024, 1025, 1024)]

        def k1_chunk(f):
            wb, wkey = wk0 if f < 4 else wk1
            pre1, sg1, kTf = pre1b[f % 2], sg1b[f % 2], kTfb[f % 2]
            pk, sk, tk = "pre%d" % (f % 2), "sg%d" % (f % 2), "kTf%d" % (f % 2)
            return fm_chunk_gen(wb, wkey, (f % 4) * 128, [(xmB, ["xmB"], rngB), (xmC, ["xmC"], [(0, 256, 1025)])],
                                pre1, pk, sg1, sk, 8 + f, [(1, 1025, True, False), (1025, 1281, False, False)], 1, 1281, True,
                                kTf[:, 1:1281], tk,
                                tr=(kTf, 1, 10, lambda j0, n, f=f: ktok_oc[:, j0:j0 + n, f * 128:(f + 1) * 128], "ktok_oc"))

        run_il([k1_chunk(f) for f in range(8)], width=2)
        WS.done()
        wv0, kv0 = WS.next()
        wv1, kv1 = WS.next()

        def v_tile(tt):
            xT, xk, col = (xmB, "xmB", 1 + tt * 128) if tt < 8 else (xmC, "xmC", (tt - 8) * 128)
            for vb, (wb, wkey) in enumerate(((wv0, kv0), (wv1, kv1))):
                pi = P.next_ps()
                for kc in range(16):
                    P.op("pe", lambda e, kc=kc, pi=pi, wb=wb: e.matmul(
                        ps[pi][:], lhsT=xT[:, kc, col:col + 128], rhs=wb[:, kc, :], start=(kc == 0), stop=(kc == 15)),
                        reads=[xk, wkey], writes=["ps%d" % pi])
                P.op("act", lambda e, pi=pi, vb=vb: e.copy(out=vext_oc[:, tt, 2 * vb:2 * vb + 2, 0:256],
                                                           in_=ps[pi][:].rearrange("p (a b) -> p a b", a=2)),
                     reads=["ps%d" % pi], writes=["vext_oc%d" % tt])

        def su(d, tt, gt):
            run_il([state_update_g(ktok_oc[:, tt, h * 256:(h + 1) * 256], "ktok_oc", vext_oc[:, tt, h, :],
                                   ["vext_oc", "vext_oc%d" % tt], gt, d, h, h) for h in range(4)], width=4)

        tile_order = [8, 9, 7, 6, 5, 4, 3, 2, 1, 0]
        due = {8: [(0, 8, 16)], 9: [(0, 9, 17), (1, 9, 17), (1, 8, 16)]}
        for j in range(8):
            due[7 - j] = [(1, 7 - j, 15 - j)]
        prev = None
        for tt in tile_order:
            v_tile(tt)
            if prev is not None:
                for args in due[prev]:
                    su(*args)
            prev = tt
        for args in due[prev]:
            su(*args)
        WS.done()
        for idx in range(8):
            cast_state(idx)
        dump("C32", C32[:].rearrange("p a b c -> p (a b c)"), [128, 8 * 2 * 257], ["C%d" % i for i in range(8)])
        dump("G", G[:].rearrange("p a b -> p (a b)"), [128, 18 * 16], ["G%d" % i for i in range(8, 18)])
        P.barrier()

        A.top = m1
        yT = A.alloc([128, 16, 1024], BF16)
        m2y = A.top
        xmA = A.alloc([128, 16, 1025], BF16)
        m2 = A.top
        xts = [A.alloc([128, 16, 128], F32) for _ in range(4)]
        f2prog = {"t": 0}

        def front2_gen():
            for j in range(8):
                k = load_x(j % 4, xoT_d[j], xts)
                make_xmT_pre(xts[j % 4], k, xmA, "xmA", j * 128, 0, extra_key="xmAt%d" % j)
                f2prog["t"] = j + 1
                yield
            k = load_x(0, xothT_d[0], xts)
            make_xmT_pre(xts[0], k, xmA, "xmA", 1024, 0, keep=(0, 1))
            yield

        def gated_gates2(i):
            while f2prog["t"] <= i:
                yield
            yield from gates_tile_g(xmA, ["xmAt%d" % i], i * 128, i)

        run_il([front2_gen()] + [gated_gates2(i) for i in range(8)], width=5)
        P.barrier()
        A.top = m2
        rowM = A.alloc([128, 2048], F32)
        load_rows(rowM, "rowM", 6)
        pre2 = [A.alloc([128, 1025], F32) for _ in range(2)]
        sg2 = [A.alloc([128, 1025], F32) for _ in range(2)]
        qT = A.alloc([128, 2, 1024], BF16)
        kT = A.alloc([128, 2, 1024], BF16)
        ktok = A.alloc([128, 8, 256], BF16)
        vext = A.alloc([128, 8, 257], BF16)
        hacc = A.alloc([128, 8, 256], F32)
        Dm2 = [A.alloc([128, 128], F32) for _ in range(2)]
        Arow2 = [A.alloc([128, 128], F32) for _ in range(2)]
        LFb2 = [A.alloc([128, 128], F32) for _ in range(2)]
        ST4 = [[A.alloc([128, 128], BF16) for _ in range(2)] for _ in range(2)]
        qA4 = [[A.alloc([128, 2, 128], BF16) for _ in range(2)] for _ in range(2)]
        hbuf2 = [A.alloc([128, 256], F32) for _ in range(2)]
        t256_2 = [A.alloc([128, 256], F32) for _ in range(2)]
        ym2 = [A.alloc([128, 256], BF16) for _ in range(2)]
        uf1 = A.alloc([128, 256], F32)
        rT1 = A.alloc([128, 2, 128], BF16)
        zs1 = A.alloc([128, 256], F32)
        wbufs = [(A.alloc([128, 16, 256], BF16), "w%d" % i) for i in range(4)]
        P.op("dve", lambda e: e.memset(vext[:, :, 256:257], 1.0), writes=["vext"])
        seq = []
        for h in range(4):
            c0 = (h % 2) * 256
            seq += [(0 + h // 2, c0, 256, 0), (2 + h // 2, c0, 256, 0), (4 + h // 2, c0, 256, 0),
                    (10 + h // 2, c0, 256, 0), (12 + h // 2, c0, 256, 0), (6 + h // 2, c0, 256, 0), (8 + h // 2, c0, 256, 0)]
        WS = WStream(wbufs, seq)

        def tail_chain(p, src_ap, src_keys, row_c0, fo, j, zb, kz):
            hb, yb = hbuf2[p], ym2[p]
            kh, ky = "hh%d" % p, "ym%d" % p
            P.op("dve", lambda e: e.tensor_tensor(out=hb[:], in0=src_ap, in1=rowM[:, row_c0:row_c0 + 256], op=ALU.mult),
                 reads=["rowM"] + list(src_keys), writes=[kh])
            yield
            P.op("dve", lambda e: e.tensor_tensor(out=yb[:], in0=hb[:], in1=zb[:], op=ALU.mult), reads=[kh, kz], writes=[ky])
            yield
            transposes_to(yb, ky, 0, 2, lambda j0, n: yT[:, fo:fo + 2, j * 128:(j + 1) * 128], "yT%d" % j)
            yield

        def proj2(po, j, wa, ka, wb2, kb2):
            for (wb_, wk2, oc) in ((wa, ka, 0), (wb2, kb2, 256)):
                for kc in range(16):
                    P.op("pe", lambda e, kc=kc, wb_=wb_, oc=oc: e.matmul(
                        ps[po][:, oc:oc + 256], lhsT=xmA[:, kc, j * 128:(j + 1) * 128], rhs=wb_[:, kc, 0:256],
                        start=(kc == 0), stop=(kc == 15)), reads=["xmA", wk2], writes=["ps%d" % po])

        def head_chain(h, j, wo_, okey, wz_, zkey):
            p = j % 2
            hb, tb = hbuf2[p], t256_2[p]
            kh, kt, ksm, kst = "hh%d" % p, "t256_%d" % p, "smallE%d" % p, "statsE%d" % p
            c = 16 + 8 * p
            po = P.next_ps()
            proj2(po, j, wo_, okey, wz_, zkey)
            yield
            P.op("act", lambda e: e.activation(out=tb[:], in_=ps[po][:, 0:256], func=AF.Sigmoid), reads=["ps%d" % po], writes=[kt])
            yield
            if h == 0:
                dump("hsum%d" % j, hacc[:, j, :], [128, 256], ["hacc%d" % j])
            P.op("dve", lambda e: e.tensor_tensor(out=hb[:], in0=hacc[:, j, :], in1=tb[:], op=ALU.mult),
                 reads=[kt, "hacc%d" % j], writes=[kh])
            yield
            P.op("act", lambda e: e.activation(out=tb[:], in_=ps[po][:, 256:512], func=AF.Silu), reads=["ps%d" % po], writes=[kt])
            yield
            P.op("dve", lambda e: e.bn_stats(out=stats[:, 4 + p, :], in_=hb[:]), reads=[kh], writes=[kst])
            yield
            P.op("dve", lambda e: e.bn_aggr(out=small[:, c:c + 2], in_=stats[:, 4 + p, :]), reads=[kst], writes=[ksm])
            yield
            P.op("act", lambda e: e.activation(out=small[:, c + 2:c + 3], in_=small[:, c + 1:c + 2], func=AF.Sqrt, bias=EPS), writes=[ksm])
            yield
            P.op("dve", lambda e: e.reciprocal(out=small[:, c + 3:c + 4], in_=small[:, c + 2:c + 3]), writes=[ksm])
            yield
            P.op("dve", lambda e: e.scalar_tensor_tensor(out=small[:, c + 4:c + 5], in0=small[:, c:c + 1], scalar=-1.0,
                                                         in1=small[:, c + 3:c + 4], op0=ALU.mult, op1=ALU.mult), writes=[ksm])
            yield
            P.op("act", lambda e: e.activation(out=hb[:], in_=hb[:], func=AF.Identity, scale=small[:, c + 3:c + 4],
                                               bias=small[:, c + 4:c + 5]), reads=[ksm], writes=[kh])
            yield
            yield from tail_chain(p, hb[:], [kh], h * 256, 2 * h, j, tb, kt)

        def pool_chain(g, j, wu_, ukey, wz_, zkey):
            p = j % 2
            ub, rb = uf1, rT1
            ku, kr = "uf", "rT"
            po = P.next_ps()
            for (wb_, wk2, oc) in ((wu_, ukey, 0), (wz_, zkey, 256)):
                for kc in range(16):
                    P.op("pe", lambda e, kc=kc, wb_=wb_, oc=oc: e.matmul(
                        ps[po][:, oc:oc + 256], lhsT=xmA[:, kc, j * 128:(j + 1) * 128], rhs=wb_[:, kc, 0:256],
                        start=(kc == 0), stop=(kc == 15)), reads=["xmA", wk2], writes=["ps%d" % po])
                yield
            P.op("act", lambda e: e.copy(out=ub[:], in_=ps[po][:, 0:256]), reads=["ps%d" % po], writes=[ku])
            yield
            P.op("act", lambda e: e.activation(out=zs1[:], in_=ps[po][:, 256:512], func=AF.Silu), reads=["ps%d" % po], writes=["zs"])
            yield
            pr = P.next_ps()
            for c2 in range(2):
                P.op("pe", lambda e, c2=c2: e.matmul(ps[pr][:, c2 * 128:(c2 + 1) * 128], lhsT=ub[:, c2 * 128:(c2 + 1) * 128],
                                                     rhs=pmT[:, g, :], start=True, stop=True), reads=[ku, "pmT"], writes=["ps%d" % pr])
            yield
            P.op("act", lambda e: e.copy(out=rb[:], in_=ps[pr][:, 0:256].rearrange("p (a b) -> p a b", a=2)),
                 reads=["ps%d" % pr], writes=[kr])
            yield
            py = P.next_ps()
            for c2 in range(2):
                P.op("pe", lambda e, c2=c2: e.matmul(ps[py][:, 0:256], lhsT=rb[:, c2, :], rhs=poolw[:, g, c2, :],
                                                     start=(c2 == 0), stop=(c2 == 1)), reads=[kr, "poolw"], writes=["ps%d" % py])
            yield
            yield from tail_chain(p, ps[py][:, 0:256], ["ps%d" % py], 1024 + g * 256, 8 + 2 * g, j, zs1, "zs")

        def scan_prep(h, d, prog):
            idx = d * 4 + h
            Dm, Arow, LFb = Dm2[d], Arow2[d], LFb2[d]
            kD, kA, kL = "Dm%d" % d, "Arow%d" % d, "LFb%d" % d
            for step in range(8):
                while step >= prog["rec%d" % d] + 2:
                    yield
                r = step % 2
                ST, qA = ST4[d][r], qA4[d][r]
                kS, kQ = "ST%d_%d" % (d, r), "qA%d_%d" % (d, r)
                j = step if d == 0 else 7 - step
                jc = slice(j * 128, (j + 1) * 128)
                P.op("dve", lambda e, j=j: e.tensor_scalar(out=LFb[:], in0=ones_f[:], scalar1=G[:, j, 8 + idx:9 + idx],
                                                            scalar2=None, op0=ALU.mult), reads=["ones_f", "G%d" % j], writes=[kL])
                yield
                pb = P.next_ps()
                P.op("pe", lambda e, pb=pb: e.matmul(ps[pb][:, 0:128], lhsT=LFb[:], rhs=tri[:, d, :], start=True, stop=True),
                     reads=[kL, "tri"], writes=["ps%d" % pb])
                yield
                P.op("act", lambda e, pb=pb, j=j: e.activation(out=Dm[:], in_=ps[pb][:, 0:128], func=AF.Exp, bias=EB[:, j, idx:idx + 1]),
                     reads=["ps%d" % pb, "G%dq" % j], writes=[kD])
                yield
                P.op("act", lambda e, pb=pb: e.activation(out=Arow[:], in_=ps[pb][:, 0:128], func=AF.Exp),
                     reads=["ps%d" % pb], writes=[kA])
                yield
                P.op("dve", lambda e: e.tensor_tensor(out=Dm[:], in0=Dm[:], in1=tri[:, 2 + d, :], op=ALU.mult), reads=["tri"], writes=[kD])
                yield
                for kcc in range(2):
                    P.op("dve", lambda e, kcc=kcc, jc=jc, qA=qA: e.tensor_tensor(out=qA[:, kcc, :], in0=qT[:, kcc, jc], in1=Arow[:], op=ALU.mult),
                         reads=["qT", kA], writes=[kQ])
                    yield
                pq = P.next_ps()
                for kcc in range(2):
                    P.op("pe", lambda e, kcc=kcc, jc=jc, pq=pq: e.matmul(ps[pq][:, 0:128], lhsT=kT[:, kcc, jc], rhs=qT[:, kcc, jc],
                                                                         start=(kcc == 0), stop=(kcc == 1)),
                         reads=["kT", "qT"], writes=["ps%d" % pq])
                yield
                P.op("dve", lambda e, pq=pq, ST=ST: e.tensor_tensor(out=ST[:], in0=ps[pq][:, 0:128], in1=Dm[:], op=ALU.mult),
                     reads=["ps%d" % pq, kD], writes=[kS])
                prog["prep%d" % d] = step + 1
                yield

        def scan_rec(h, d, prog):
            idx = d * 4 + h
            kSm = "small%d" % d
            s0 = 8 + 2 * d
            for step in range(8):
                while prog["prep%d" % d] <= step:
                    yield
                r = step % 2
                ST, qA = ST4[d][r], qA4[d][r]
                kS, kQ = "ST%d_%d" % (d, r), "qA%d_%d" % (d, r)
                j = step if d == 0 else 7 - step
                ph = P.next_ps()
                P.op("pe", lambda e, ph=ph, j=j, ST=ST: e.matmul(ps[ph][:, 0:257], lhsT=ST[:], rhs=vext[:, j, :], start=True, stop=False),
                     reads=[kS, "vext"], writes=["ps%d" % ph])
                for kcc in range(2):
                    P.op("pe", lambda e, ph=ph, kcc=kcc, qA=qA: e.matmul(ps[ph][:, 0:257], lhsT=qA[:, kcc, :], rhs=C16[:, idx, kcc, :],
                                                                         start=False, stop=(kcc == 1)),
                         reads=[kQ, "C16_%d" % idx], writes=["ps%d" % ph])
                prog["rec%d" % d] = step + 1
                yield
                P.op("dve", lambda e, ph=ph: e.tensor_scalar(out=small[:, s0:s0 + 1], in0=ps[ph][:, 256:257], scalar1=-1.0, scalar2=None,
                                                             op0=ALU.mult), reads=["ps%d" % ph], writes=[kSm])
                yield
                P.op("dve", lambda e, ph=ph: e.scalar_tensor_tensor(out=small[:, s0:s0 + 1], in0=small[:, s0:s0 + 1], scalar=1.0,
                                                                    in1=ps[ph][:, 256:257], op0=ALU.max, op1=ALU.max),
                     reads=["ps%d" % ph], writes=[kSm])
                yield
                P.op("dve", lambda e: e.reciprocal(out=small[:, s0 + 1:s0 + 2], in_=small[:, s0:s0 + 1]), writes=[kSm])
                yield
                first = j not in prog["hw"]
                prog["hw"].add(j)
                if first:
                    P.op("act", lambda e, ph=ph, j=j: e.activation(out=hacc[:, j, :], in_=ps[ph][:, 0:256], func=AF.Identity,
                                                                    scale=small[:, s0 + 1:s0 + 2]),
                         reads=["ps%d" % ph, kSm], writes=["hacc%d" % j])
                else:
                    P.op("dve", lambda e, ph=ph, j=j: e.scalar_tensor_tensor(
                        out=hacc[:, j, :], in0=ps[ph][:, 0:256], scalar=small[:, s0 + 1:s0 + 2], in1=hacc[:, j, :],
                        op0=ALU.mult, op1=ALU.add), reads=["ps%d" % ph, kSm], writes=["hacc%d" % j])
                yield
                if step < 7:
                    yield from state_update_g(ktok[:, j, :], "ktok", vext[:, j, :], "vext", j, d, h, d)
                    cast_state(idx)
                    yield

        rng3 = [(0, 512, 0), (512, 1024, 512), (1024, 1025, 1024)]
        for h in range(4):
            wq, qkey = WS.next()
            wk_, kkey = WS.next()

            def qk_chunk(i):
                f2 = i % 2
                pre, pk, sgx, sk = pre2[f2], "pre%d" % f2, sg2[f2], "sgh%d" % f2
                if i < 2:
                    return fm_chunk_gen(wq, qkey, f2 * 128, [(xmA, ["xmA"], rng3)], pre, pk, sgx, sk, 2 * h + f2,
                                        [(0, 1024, False, True)], 0, 1024, False, qT[:, f2, :], "qT")
                return fm_chunk_gen(wk_, kkey, f2 * 128, [(xmA, ["xmA"], rng3)], pre, pk, sgx, sk, 8 + 2 * h + f2,
                                    [(0, 1024, False, True)], 0, 1024, True, kT[:, f2, :], "kT",
                                    tr=(kT[:, f2, :], 0, 8, lambda j0, n, f2=f2: ktok[:, j0:j0 + n, f2 * 128:(f2 + 1) * 128], "ktok"))

            run_il([qk_chunk(i) for i in range(4)], width=2)
            WS.done()
            wv, vkey = WS.next()
            for j in range(8):
                pi = P.next_ps()
                for kc in range(16):
                    P.op("pe", lambda e, kc=kc, pi=pi, j=j, wv=wv: e.matmul(
                        ps[pi][:, 0:256], lhsT=xmA[:, kc, j * 128:(j + 1) * 128], rhs=wv[:, kc, 0:256],
                        start=(kc == 0), stop=(kc == 15)), reads=["xmA", vkey], writes=["ps%d" % pi])
                P.op("act", lambda e, pi=pi, j=j: e.copy(out=vext[:, j, 0:256], in_=ps[pi][:, 0:256]), reads=["ps%d" % pi], writes=["vext"])
            WS.done()
            wu_, ukey = WS.next()
            wz_, zkey = WS.next()
            prog = {"prep0": 0, "prep1": 0, "rec0": 0, "rec1": 0, "hw": set()}
            run_il([scan_prep(h, 0, prog), scan_prep(h, 1, prog), scan_rec(h, 0, prog), scan_rec(h, 1, prog)], width=4,
                   bg=[pool_chain(h, j, wu_, ukey, wz_, zkey) for j in range(8)])
            WS.done()
            wo_, okey = WS.next()
            wz_, zkey = WS.next()
            run_il([head_chain(h, j, wo_, okey, wz_, zkey) for j in range(8)])
            WS.done()
        dump("yT", yT[:].rearrange("p a b -> p (a b)"), [128, 16 * 1024], ["yT%d" % j for j in range(8)])
        P.barrier()
        A.top = mg
        rowA = A.alloc([128, 2048], F32)
        rowB = A.alloc([128, 2048], F32)
        assert A.top <= m1, A.top
        A.top = m2y
        tmp512 = [A.alloc([128, 512], F32) for _ in range(2)]
        wbufs = [(A.alloc([128, 16, 512], BF16), "w%d" % i) for i in range(3)]
        assert A.top <= XR_OFF, A.top
        for t in range(8):
            P.dma("sp", "xr%d" % t, lambda e, t=t: e.dma_start(out=XR[:, t, :], in_=xo[t * 128:(t + 1) * 128, :]),
                  writes=["XR%d" % t])
        load_rows(rowA, "rowg", 0)
        load_rows(rowB, "rowb", 1)
        WS = WStream(wbufs, [(14 + nb, 0, 512) for nb in range(4)])
        out_proj(WS, yT, "yT%d", list(range(8)), g_row, tmp512,
                 after_tile=lambda t: layer_norm_tile(t, rowA, rowB, None, store=False))
        P.barrier()

    if L1:
        A.top = base_top
        g_row = A.alloc([128, 2048], F32)
        ada(1, g_row)
        dump("mod1", mods[1][:].rearrange("p a b -> p (a b)"), [128, 96], ["mod"])
        dump("grow1", g_row[:], [128, 2048], ["g_row"])
        rowA = A.alloc([128, 2048], F32)
        rowB = A.alloc([128, 2048], F32)
        wspT = A.alloc([128, 8, 128], BF16)
        bsp = A.alloc([128, 8], F32)
        P.dma("pool", "c3", lambda e: e.dma_start(out=wspT[:].rearrange("p a b -> p (a b)"), in_=wsp_d[:, :]), writes=["wspT"])
        P.dma("sp", "c4", lambda e: e.dma_start(out=bsp[:], in_=bsp_d[:, :]), writes=["bsp"])
        wbufs = [(A.alloc([128, 16, 512], BF16), "w%d" % i) for i in range(3)]
        xg = A.alloc([128, 16, 512], BF16)
        VS = A.alloc([128, 4, 2048], F32)
        vln2 = [A.alloc([128, 2048], BF16) for _ in range(2)]
        tmp512 = [A.alloc([128, 512], F32) for _ in range(2)]
        assert A.top <= XR_OFF, A.top
        for gi in range(2):
            tiles = [4 * gi + i for i in range(4)]
            seq = [(WB1 + 4 + nb, 0, 512) for nb in range(4)]
            for nb in range(4):
                seq += [(WB1 + nb, 0, 512), (WB1 + 8 + nb, 0, 512)]
            seq += [(WB1 + 12 + nb, 0, 512) for nb in range(4)]
            WS = WStream(wbufs, seq)
            for ti, t in enumerate(tiles):
                make_xmT(XR[:, t, :], "XR%d" % t, xg, "xg%d" % ti, ti * 128, 0)
            if gi == 0:
                dump("xg", xg[:].rearrange("p a b -> p (a b)"), [128, 8192], ["xg0", "xg1", "xg2", "xg3"])
            load_rows(rowA, "rowg", 4)
            load_rows(rowB, "rowb", 5)
            for nb in range(4):
                wb, wkey = WS.next()
                for ti in range(4):
                    pi = P.next_ps()
                    for kc in range(16):
                        P.op("pe", lambda e, kc=kc, ti=ti, pi=pi, wb=wb: e.matmul(
                            ps[pi][:], lhsT=xg[:, kc, ti * 128:(ti + 1) * 128], rhs=wb[:, kc, :],
                            start=(kc == 0), stop=(kc == 15)), reads=["xg%d" % ti, wkey], writes=["ps%d" % pi])
                    P.op("act", lambda e, ti=ti, nb=nb, pi=pi: e.copy(out=VS[:, ti, nb * 512:(nb + 1) * 512], in_=ps[pi][:]),
                         reads=["ps%d" % pi], gwrites=["VS%d" % ti])
                WS.done()
            if gi == 0:
                dump("vpre", VS[:].rearrange("p a b -> p (a b)"), [128, 8192], ["VS0", "VS1", "VS2", "VS3"])
            def vln_chain(ti):
                p = ti % 2
                vk, ksm, kst, kv = "VS%d" % ti, "smallV%d" % p, "statsV%d" % p, "vln%d" % p
                c0 = 24 + 8 * p
                vl = vln2[p]
                for c in range(4):
                    P.op("dve", lambda e, c=c: e.bn_stats(out=stats[:, 8 + 4 * p + c, :], in_=VS[:, ti, c * 512:(c + 1) * 512]),
                         reads=[vk], writes=[kst])
                yield
                P.op("dve", lambda e: e.bn_aggr(out=small[:, c0:c0 + 2], in_=stats[:, 8 + 4 * p:12 + 4 * p, :].rearrange("p a b -> p (a b)")),
                     reads=[kst], writes=[ksm])
                yield
                P.op("act", lambda e: e.activation(out=small[:, c0 + 2:c0 + 3], in_=small[:, c0 + 1:c0 + 2], func=AF.Sqrt, bias=EPS), writes=[ksm])
                yield
                P.op("dve", lambda e: e.reciprocal(out=small[:, c0 + 3:c0 + 4], in_=small[:, c0 + 2:c0 + 3]), writes=[ksm])
                yield
                P.op("dve", lambda e: e.scalar_tensor_tensor(out=small[:, c0 + 4:c0 + 5], in0=small[:, c0:c0 + 1], scalar=-1.0,
                                                             in1=small[:, c0 + 3:c0 + 4], op0=ALU.mult, op1=ALU.mult), writes=[ksm])
                yield
                P.op("act", lambda e: e.activation(out=VS[:, ti, :], in_=VS[:, ti, :], func=AF.Identity, scale=small[:, c0 + 3:c0 + 4],
                                                   bias=small[:, c0 + 4:c0 + 5]), reads=[ksm], writes=[vk])
                yield
                P.op("dve", lambda e: e.tensor_tensor(out=VS[:, ti, :], in0=VS[:, ti, :], in1=rowA[:], op=ALU.mult),
                     reads=["rowg"], writes=[vk])
                yield
                P.op("dve", lambda e: e.tensor_tensor(out=vl[:], in0=VS[:, ti, :], in1=rowB[:], op=ALU.add),
                     reads=[vk, "rowb"], writes=[kv])
                yield
                for hp in range(4):
                    pi = P.next_ps()
                    for hh in range(2):
                        h = 2 * hp + hh
                        P.op("pe", lambda e, h=h, hh=hh, pi=pi: e.matmul(
                            ps[pi][:, hh * 256:(hh + 1) * 256], lhsT=wspT[:, h, :], rhs=vl[:, h * 256:(h + 1) * 256],
                            start=True, stop=True), reads=["wspT", kv], writes=["ps%d" % pi])
                    yield
                    for hh in range(2):
                        h = 2 * hp + hh
                        P.op("act" if hh == 0 else "dve",
                             (lambda e, h=h, hh=hh, pi=pi: e.activation(
                                 out=VS[:, ti, h * 256:(h + 1) * 256], in_=ps[pi][:, hh * 256:(hh + 1) * 256],
                                 func=AF.Identity, bias=bsp[:, h:h + 1])) if hh == 0 else
                             (lambda e, h=h, hh=hh, pi=pi: e.tensor_scalar(
                                 out=VS[:, ti, h * 256:(h + 1) * 256], in0=ps[pi][:, hh * 256:(hh + 1) * 256],
                                 scalar1=bsp[:, h:h + 1], scalar2=None, op0=ALU.add)),
                             reads=["ps%d" % pi, "bsp"], gwrites=[vk])
                    yield

            run_il([vln_chain(ti) for ti in range(4)], width=2)
            if gi == 0:
                dump("s", VS[:].rearrange("p a b -> p (a b)"), [128, 8192], ["VS0", "VS1", "VS2", "VS3"])
            for nb in range(4):
                wu, ukey = WS.next()
                pus = []
                for ti in range(4):
                    pu = P.next_ps()
                    pus.append(pu)
                    for kc in range(16):
                        P.op("pe", lambda e, kc=kc, ti=ti, pu=pu, wu=wu: e.matmul(
                            ps[pu][:], lhsT=xg[:, kc, ti * 128:(ti + 1) * 128], rhs=wu[:, kc, :],
                            start=(kc == 0), stop=(kc == 15)), reads=["xg%d" % ti, ukey], writes=["ps%d" % pu])
                WS.done()
                wz, zkey = WS.next()
                for ti in range(4):
                    vk = "VS%d" % ti
                    pu = pus[ti]
                    pz = P.next_ps()
                    for kc in range(16):
                        P.op("pe", lambda e, kc=kc, ti=ti, pz=pz, wz=wz: e.matmul(
                            ps[pz][:], lhsT=xg[:, kc, ti * 128:(ti + 1) * 128], rhs=wz[:, kc, :],
                            start=(kc == 0), stop=(kc == 15)), reads=["xg%d" % ti, zkey], writes=["ps%d" % pz])
                    t5, t5k = tmp512[ti % 2], "tmp512_%d" % (ti % 2)
                    P.op("act", lambda e, pz=pz, t5=t5: e.activation(out=t5[:], in_=ps[pz][:], func=AF.Silu),
                         reads=["ps%d" % pz], writes=[t5k])
                    P.op("dve", lambda e, ti=ti, nb=nb, pu=pu: e.tensor_tensor(
                        out=VS[:, ti, nb * 512:(nb + 1) * 512], in0=ps[pu][:], in1=VS[:, ti, nb * 512:(nb + 1) * 512],
                        op=ALU.mult), reads=["ps%d" % pu], writes=[vk])
                    P.op("dve", lambda e, ti=ti, nb=nb, t5=t5: e.tensor_tensor(
                        out=VS[:, ti, nb * 512:(nb + 1) * 512], in0=VS[:, ti, nb * 512:(nb + 1) * 512], in1=t5[:],
                        op=ALU.mult), reads=[t5k], writes=[vk])
                WS.done()
            if gi == 0:
                dump("y", VS[:].rearrange("p a b -> p (a b)"), [128, 8192], ["VS0", "VS1", "VS2", "VS3"])
            for ti in range(4):
                for q4 in range(4):
                    pi = P.next_ps()
                    for j4 in range(4):
                        kc = q4 * 4 + j4
                        P.op("pe", lambda e, kc=kc, j4=j4, pi=pi, ti=ti: e.transpose(
                            out=ps[pi][:, j4 * 128:(j4 + 1) * 128], in_=VS[:, ti, kc * 128:(kc + 1) * 128], identity=ident_f[:]),
                            reads=["VS%d" % ti, "ident_f"], writes=["ps%d" % pi])
                    P.op("act" if q4 % 2 == 0 else "dve",
                         (lambda e, q4=q4, pi=pi, ti=ti: e.copy(
                             out=xg[:, q4 * 4:(q4 + 1) * 4, ti * 128:(ti + 1) * 128],
                             in_=ps[pi][:].rearrange("p (a b) -> p a b", a=4))) if q4 % 2 == 0 else
                         (lambda e, q4=q4, pi=pi, ti=ti: e.tensor_copy(
                             out=xg[:, q4 * 4:(q4 + 1) * 4, ti * 128:(ti + 1) * 128],
                             in_=ps[pi][:].rearrange("p (a b) -> p a b", a=4))),
                         reads=["ps%d" % pi], gwrites=["xg%d" % ti])
            if gi == 0:
                dump("pre", XR[:, 0:4, :].rearrange("p a b -> p (a b)"), [128, 8192], ["XR0", "XR1", "XR2", "XR3"])
            load_rows(rowA, "rowg", 2)
            load_rows(rowB, "rowb", 3)
            out_proj(WS, xg, "xg%d", tiles, g_row, tmp512,
                     after_tile=lambda t: layer_norm_tile(t, rowA, rowB, None, store=True))
    elif L0:
        for t in range(8):
            P.dma("sp", "st%d" % (t % 2), lambda e, t=t: e.dma_start(out=out_d[t * 128:(t + 1) * 128, :], in_=XR[:, t, :]),
                  reads=["XR%d" % t], final=True)

    P.finalize()
    return nc


def _blk(w):
    return np.ascontiguousarray(w.reshape(16, 128, 512).transpose(1, 0, 2)).reshape(128, 8192)


def _prep_shared(inp):
    f = np.float32
    sh = {}
    sh["ident"] = np.eye(128, dtype=f)
    aw = inp["ada_w"]
    sh["adaw"] = np.ascontiguousarray(
        aw.reshape(2, 16, 128, 12, 512).transpose(0, 3, 2, 1, 4)).reshape(24, 128, 8192)
    ab = inp["ada_b"].reshape(2, 48, 128).transpose(2, 0, 1)
    sh["adabf"] = np.ascontiguousarray(np.repeat(ab[:, :, :, None], 2, axis=3)).reshape(128, 192)
    rows = np.zeros((10, 2048), f)
    rows[0] = inp["post_ln_g"][0]
    rows[1] = inp["post_ln_b"][0]
    rows[2] = inp["post_ln_g"][1]
    rows[3] = inp["post_ln_b"][1]
    rows[4] = inp["sgu_ln_g"][0]
    rows[5] = inp["sgu_ln_b"][0]
    rows[6, :1024] = inp["mh_norm_g"][0]
    rows[6, 1024:] = inp["pool_scale"][0]
    sh["rowsb"] = np.ascontiguousarray(np.broadcast_to(rows[:, None, :], (10, 128, 2048)))
    we = inp["w_in_even"][0]
    cols = list(range(0, 5120)) + list(range(5136, 7184))
    wes = we[:, cols]
    blocks = [_blk(wes[:, i * 512:(i + 1) * 512]) for i in range(14)]
    blocks += [_blk(inp["w_out_even"][0][:, i * 512:(i + 1) * 512]) for i in range(4)]
    wo = inp["w_in_odd"][0]
    blocks += [_blk(wo[:, i * 512:(i + 1) * 512]) for i in range(12)]
    blocks += [_blk(inp["w_out_odd"][0][:, i * 512:(i + 1) * 512]) for i in range(4)]
    sh["wst"] = np.stack(blocks)
    return sh


def _prep_core(inp, sh, b, half, layers, x_override=None):
    f = np.float32
    flip = half == 1
    m = dict(sh)
    xsrc = inp["x"] if x_override is None else x_override
    xb = xsrc[b][::-1] if flip else xsrc[b]
    m["xo"] = np.ascontiguousarray(xb[:1024])
    cc = np.stack([inp["c"][b], inp["c_ctx"]], axis=-1)
    m["cc"] = np.ascontiguousarray(cc.reshape(16, 128, 2).transpose(1, 0, 2)).reshape(128, 32)
    if 0 in layers:
        cb = inp["ctx"][b][::-1] if flip else inp["ctx"][b]

        def tiles_T(a):
            n = a.shape[0] // 128
            return np.ascontiguousarray(a.reshape(n, 128, 16, 128).transpose(0, 3, 2, 1)).reshape(n, 128, 2048)
        m["xoT"] = tiles_T(xb[:1024])
        m["xothT"] = tiles_T(xb[1024:])
        m["xcT"] = tiles_T(cb)
        m.update(_prep_l0_consts(inp, flip))
    if 1 in layers:
        wsp = inp["w_sp"][0]
        bsp = inp["b_sp"][0]
        if flip:
            wsp = wsp[:, ::-1, ::-1]
            bsp = bsp[:, ::-1]
        m["wspT"] = np.ascontiguousarray(wsp.transpose(2, 0, 1)).reshape(128, 1024)
        m["bsp"] = np.ascontiguousarray(bsp.T)
    return m


def _prep_l0_consts(inp, flip):
    f = np.float32
    m = {}
    we = inp["w_in_even"][0]
    gcols = []
    for i_f in range(2):
        for ld in range(2):
            gd = (1 - ld) if flip else ld
            for h in range(4):
                gcols.append(5120 + (i_f * 2 + gd) * 4 + h)
    wgm = we[:, gcols]
    m["wg"] = np.ascontiguousarray(wgm.reshape(16, 128, 16).transpose(1, 0, 2)).reshape(128, 256)
    bi, bf = inp["b_igate"][0], inp["b_fgate"][0]
    if flip:
        bi, bf = bi[::-1], bf[::-1]
    gb = np.concatenate([bi.reshape(-1), bf.reshape(-1)]).astype(f)
    m["gbias"] = np.ascontiguousarray(np.broadcast_to(gb[None, :], (128, 16)))
    cw = inp["conv_qk"][0]
    if flip:
        cw = cw[::-1]
    m["convw"] = np.ascontiguousarray(cw.T.reshape(16, 128, 3).transpose(1, 0, 2)).reshape(128, 48)
    p = np.arange(128)
    le = (p[:, None] <= p[None, :]).astype(f)
    ge = (p[:, None] >= p[None, :]).astype(f)
    m["tri"] = np.stack([le, ge, le, ge])
    pm = np.zeros((4, 128, 128), f)
    pos = np.arange(64)
    for g, w in enumerate((2, 4, 8, 16)):
        lo = np.maximum(pos - w // 2, 0)
        hi = np.minimum(pos + (w - 1 - w // 2), 63)
        M = np.zeros((64, 64), f)
        for t in range(64):
            M[t, lo[t]:hi[t] + 1] = f(1.0) / f(hi[t] - lo[t] + 1)
        M -= np.eye(64, dtype=f)
        pm[g, :64, :64] = M
        pm[g, 64:, 64:] = M
        if flip:
            pm[g] = pm[g][::-1, ::-1]
    m["pmT"] = np.ascontiguousarray(pm.transpose(0, 2, 1))
    pw = inp["pool_w"][0]
    m["poolw"] = np.ascontiguousarray(pw.reshape(4, 2, 128, 256).transpose(2, 0, 1, 3)).reshape(128, 2048)
    return m


_CACHE = {}


def _get_nc(layers):
    key = tuple(layers)
    if key not in _CACHE:
        _CACHE[key] = build(list(layers))
    return _CACHE[key]


def _run(inp, layers, x_override=None):
    sh = _prep_shared(inp)
    in_maps = []
    for r in range(8):
        b, half = r // 2, r % 2
        m = _prep_core(inp, sh, b, half, layers, x_override)
        if tuple(layers) == (0,):
            m["wst"] = sh["wst"][:NBLK0]
            m["adaw"] = sh["adaw"][:12]
            m["adabf"] = np.ascontiguousarray(sh["adabf"][:, :96])
        elif tuple(layers) == (1,):
            m["wst"] = sh["wst"][NBLK0:]
            m["adaw"] = sh["adaw"][12:]
            m["adabf"] = np.ascontiguousarray(sh["adabf"][:, 96:])
        in_maps.append(m)
    nc = _get_nc(layers)
    names = None
    res = run_bass_kernel_spmd(nc, in_maps, core_ids=list(range(8)))
    global LAST_RES
    LAST_RES = res.results
    out = np.zeros((4, 2048, 2048), np.float32)
    for r in range(8):
        b, half = r // 2, r % 2
        o = res.results[r]["out"]
        if half == 1:
            out[b, 1024:] = o[::-1]
        else:
            out[b, :1024] = o
    return out


def kernel(**inputs):
    inp = {k: np.asarray(v) for k, v in inputs.items()}
    return _run(inp, (0, 1))
```
